# Optimizing a Trainium2 kernel written in Bass

```python
import math
import jax, jax.numpy as jnp
from jax import lax
import numpy as np

D_MODEL = 1024
BATCH = 8
SEQ = 2048
DEPTH = 2

HEAD_DIM = 64
N_HEADS_MOBA = 6
N_HEADS_DIL = 6
N_HEADS_MEM = 4
D_MOBA = N_HEADS_MOBA * HEAD_DIM
D_DIL = N_HEADS_DIL * HEAD_DIM
D_MEM = N_HEADS_MEM * HEAD_DIM
D_MIX = D_MOBA + D_DIL + D_MEM
D_IN = 4 * D_MOBA + 4 * D_DIL + 2 * D_MEM
N_MEM = 256
MOBA_BLOCK = 256
MOBA_TOPK = 3
MOBA_Q_CHUNK = 32
DIL_PATTERNS = ((128, 1), (512, 4), (2048, 16))
ROPE_THETA = 10000.0
EPS = 1e-6

kernel_name = "hybrid_moba_dilated_memxattn"


def rms_norm(x, g):
    xf = x.astype(jnp.float32)
    y = xf * lax.rsqrt(jnp.mean(xf * xf, axis=-1, keepdims=True) + EPS)
    return (y * g.astype(jnp.float32)).astype(x.dtype)


def rope(x):
    T = x.shape[2]
    half = HEAD_DIM // 2
    inv = ROPE_THETA ** (-jnp.arange(half, dtype=jnp.float32) / half)
    ang = jnp.arange(T, dtype=jnp.float32)[:, None] * inv[None, :]
    cos, sin = jnp.cos(ang), jnp.sin(ang)
    x1 = x[..., :half].astype(jnp.float32)
    x2 = x[..., half:].astype(jnp.float32)
    out = jnp.concatenate([x1 * cos - x2 * sin, x2 * cos + x1 * sin], axis=-1)
    return out.astype(x.dtype)


def to_heads(t, n):
    B, T, _ = t.shape
    return t.reshape(B, T, n, HEAD_DIM).transpose(0, 2, 1, 3)


def from_heads(t):
    B, H, T, dh = t.shape
    return t.transpose(0, 2, 1, 3).reshape(B, T, H * dh)


def moba_attention(q, k, v):
    B, H, T, dh = q.shape
    nb = -(-T // MOBA_BLOCK)
    Tp = nb * MOBA_BLOCK
    pad = ((0, 0), (0, 0), (0, Tp - T), (0, 0))
    q, k, v = jnp.pad(q, pad), jnp.pad(k, pad), jnp.pad(v, pad)
    kb = k.reshape(B, H, nb, MOBA_BLOCK, dh)
    vb = v.reshape(B, H, nb, MOBA_BLOCK, dh)
    k_mean = jnp.mean(kb.astype(jnp.float32), axis=3)
    topk = min(MOBA_TOPK, nb - 1)
    n_chunks = Tp // MOBA_Q_CHUNK
    q_chunks = q.reshape(B, H, n_chunks, MOBA_Q_CHUNK, dh).transpose(2, 0, 1, 3, 4)
    b_ix = jnp.arange(B)[:, None, None, None]
    h_ix = jnp.arange(H)[None, :, None, None]
    scale = dh ** -0.5

    def one_chunk(args):
        ci, qc = args
        start = ci * MOBA_Q_CHUNK
        c = start // MOBA_BLOCK
        q_pos = start + jnp.arange(MOBA_Q_CHUNK)
        k_pos = c * MOBA_BLOCK + jnp.arange(MOBA_BLOCK)
        own_k = lax.dynamic_index_in_dim(kb, c, axis=2, keepdims=False)
        own_v = lax.dynamic_index_in_dim(vb, c, axis=2, keepdims=False)
        s_own = jnp.einsum('bhqd,bhkd->bhqk', qc, own_k).astype(jnp.float32) * scale
        s_own = jnp.where(k_pos[None, :] <= q_pos[:, None], s_own, -jnp.inf)
        if topk == 0:
            p = jax.nn.softmax(s_own, axis=-1).astype(vb.dtype)
            return jnp.einsum('bhqk,bhkd->bhqd', p, own_v)
        gate = jnp.einsum('bhqd,bhnd->bhqn', qc.astype(jnp.float32), k_mean)
        gate = jnp.where(jnp.arange(nb) < c, gate, -jnp.inf)
        _, idx = lax.top_k(gate, topk)
        valid = idx < c
        sel_k = kb[b_ix, h_ix, idx]
        sel_v = vb[b_ix, h_ix, idx]
        s_sel = jnp.einsum('bhqd,bhqnkd->bhqnk', qc, sel_k).astype(jnp.float32) * scale
        s_sel = jnp.where(valid[..., None], s_sel, -jnp.inf)
        s_sel = s_sel.reshape(B, H, MOBA_Q_CHUNK, topk * MOBA_BLOCK)
        p = jax.nn.softmax(jnp.concatenate([s_own, s_sel], axis=-1), axis=-1).astype(vb.dtype)
        p_own = p[..., :MOBA_BLOCK]
        p_sel = p[..., MOBA_BLOCK:].reshape(B, H, MOBA_Q_CHUNK, topk, MOBA_BLOCK)
        return (jnp.einsum('bhqk,bhkd->bhqd', p_own, own_v)
                + jnp.einsum('bhqnk,bhqnkd->bhqd', p_sel, sel_v))

    out = lax.map(one_chunk, (jnp.arange(n_chunks, dtype=jnp.int32), q_chunks))
    return out.transpose(1, 2, 0, 3, 4).reshape(B, H, Tp, dh)[:, :, :T]


def dilated_attention(q, k, v):
    B, H, T, dh = q.shape
    scale = dh ** -0.5
    outs, lses = [], []
    for window, dil in DIL_PATTERNS:
        W = window // dil
        n = -(-T // (W * dil))
        L = n * W * dil
        pad = ((0, 0), (0, 0), (0, L - T), (0, 0))

        def split(t):
            return jnp.pad(t, pad).reshape(B, H, n, W, dil, dh).transpose(0, 1, 4, 2, 3, 5)

        qs, ks, vs = split(q), split(k), split(v)
        prev = ((0, 0), (0, 0), (0, 0), (1, 0), (0, 0), (0, 0))
        k_cat = jnp.concatenate([jnp.pad(ks, prev)[:, :, :, :-1], ks], axis=4)
        v_cat = jnp.concatenate([jnp.pad(vs, prev)[:, :, :, :-1], vs], axis=4)
        s = jnp.einsum('bhrnqd,bhrnkd->bhrnqk', qs, k_cat).astype(jnp.float32) * scale
        i = jnp.arange(W)[:, None]
        j = jnp.arange(2 * W)[None, :]
        band = (j >= i) & (j <= i + W)
        first = (jnp.arange(n)[:, None, None] > 0) | (j[None] >= W)
        s = jnp.where(band[None] & first, s, -jnp.inf)
        lse = jax.nn.logsumexp(s, axis=-1)
        p = jnp.exp(s - lse[..., None]).astype(v.dtype)
        o = jnp.einsum('bhrnqk,bhrnkd->bhrnqd', p, v_cat)
        outs.append(o.transpose(0, 1, 3, 4, 2, 5).reshape(B, H, L, dh)[:, :, :T])
        lses.append(lse.transpose(0, 1, 3, 4, 2).reshape(B, H, L)[:, :, :T])
    w = jax.nn.softmax(jnp.stack(lses, axis=0), axis=0)
    o_all = jnp.stack(outs, axis=0).astype(jnp.float32)
    return jnp.einsum('pbht,pbhtd->bhtd', w, o_all).astype(q.dtype)


def memory_attention(q, k, v):
    s = jnp.einsum('bhqd,bhkd->bhqk', q, k).astype(jnp.float32) * (HEAD_DIM ** -0.5)
    p = jax.nn.softmax(s, axis=-1).astype(v.dtype)
    return jnp.einsum('bhqk,bhkd->bhqd', p, v)


def hybrid_layer(x, mem, norm_g, w_in, w_out, mem_norm_g, w_mem_kv, qk_g):
    h = rms_norm(x, norm_g)
    proj = jnp.einsum('btd,de->bte', h, w_in)
    sizes = [D_MOBA] * 4 + [D_DIL] * 4 + [D_MEM] * 2
    cuts = [int(c) for c in np.cumsum(sizes)[:-1]]
    a_q, a_k, a_v, a_g, b_q, b_k, b_v, b_g, m_q, m_g = jnp.split(proj, cuts, axis=-1)

    qa = rope(rms_norm(to_heads(a_q, N_HEADS_MOBA), qk_g[0]))
    ka = rope(rms_norm(to_heads(a_k, N_HEADS_MOBA), qk_g[1]))
    oa = from_heads(moba_attention(qa, ka, to_heads(a_v, N_HEADS_MOBA)))

    qb = rope(rms_norm(to_heads(b_q, N_HEADS_DIL), qk_g[2]))
    kb = rope(rms_norm(to_heads(b_k, N_HEADS_DIL), qk_g[3]))
    ob = from_heads(dilated_attention(qb, kb, to_heads(b_v, N_HEADS_DIL)))

    mkv = jnp.einsum('bmd,de->bme', rms_norm(mem, mem_norm_g), w_mem_kv)
    m_k, m_v = jnp.split(mkv, 2, axis=-1)
    qm = rms_norm(to_heads(m_q, N_HEADS_MEM), qk_g[4])
    km = rms_norm(to_heads(m_k, N_HEADS_MEM), qk_g[5])
    om = from_heads(memory_attention(qm, km, to_heads(m_v, N_HEADS_MEM)))

    y = jnp.concatenate([oa * jax.nn.silu(a_g), ob * jax.nn.silu(b_g), om * jax.nn.silu(m_g)], axis=-1)
    return x + jnp.einsum('bte,ed->btd', y, w_out)


def setup_inputs(seed: int = 0) -> dict:
    key = jax.random.key(seed)
    ks = jax.random.split(key, 10)
    f32 = jnp.float32
    x = jax.random.normal(ks[0], (BATCH, SEQ, D_MODEL), f32)
    mem = jax.random.normal(ks[1], (BATCH, N_MEM, D_MODEL), f32)
    norm_g = 1.0 + 0.02 * jax.random.normal(ks[2], (DEPTH, D_MODEL), f32)
    w_in = jax.random.normal(ks[3], (DEPTH, D_MODEL, D_IN), f32) * D_MODEL ** -0.5
    w_out = jax.random.normal(ks[4], (DEPTH, D_MIX, D_MODEL), f32) * (D_MIX ** -0.5) / math.sqrt(2 * DEPTH)
    mem_norm_g = 1.0 + 0.02 * jax.random.normal(ks[5], (DEPTH, D_MODEL), f32)
    w_mem_kv = jax.random.normal(ks[6], (DEPTH, D_MODEL, 2 * D_MEM), f32) * D_MODEL ** -0.5
    qk_g = 1.0 + 0.02 * jax.random.normal(ks[7], (DEPTH, 6, HEAD_DIM), f32)
    return {"x": x, "mem": mem, "norm_g": norm_g, "w_in": w_in, "w_out": w_out,
            "mem_norm_g": mem_norm_g, "w_mem_kv": w_mem_kv, "qk_g": qk_g}


def reference(x, mem, norm_g, w_in, w_out, mem_norm_g, w_mem_kv, qk_g):
    for layer in range(DEPTH):
        x = hybrid_layer(x, mem, norm_g[layer], w_in[layer], w_out[layer],
                         mem_norm_g[layer], w_mem_kv[layer], qk_g[layer])
    return x
```

```python
import numpy as np
from contextlib import ExitStack
import concourse.bass as bass
import concourse.mybir as mybir
from concourse.bass_utils import run_bass_kernel_spmd

F32 = mybir.dt.float32
BF16 = mybir.dt.bfloat16
AF = mybir.ActivationFunctionType
ALU = mybir.AluOpType
AX = mybir.AxisListType

T = 2048
D = 1024
DIN = 3584
NMEM = 256
NT = T // 128
NG = T // 512
EPS = 1e-6
BIG = 30000.0
N_CORES = 8

SAME_ENGINE_SYNC = True
SKEW_ON = 3
PRE_A = True
DEFER_FG = True
ATT_SKEW = 7
INTERLEAVE_V = False
PRE_V = True


class Sched:
    ENGS = ("pe", "act", "dve", "pool", "sp")

    def __init__(self, nc, es, n_dma_sp=8, n_dma_pool=20, n_dma_act=2):
        self.nc = nc
        self.ops = {e: [] for e in self.ENGS}
        self.sem = {e: es.enter_context(nc.semaphore("s_" + e)) for e in self.ENGS}
        self.cnt = {e: 0 for e in self.ENGS}
        self.waited = {e: {} for e in self.ENGS}
        self.last_w = {}
        self.readers = {}
        self.semobj = {}
        for e in self.ENGS:
            self.semobj[("c", e)] = self.sem[e]
        self.dma_slots = {}
        self.dma_rr = {}
        for q, n in (("sp", n_dma_sp), ("pool", n_dma_pool), ("act", n_dma_act)):
            sl = []
            for i in range(n):
                key = ("d", q, i)
                self.semobj[key] = es.enter_context(nc.semaphore("d_%s%d" % (q, i)))
                sl.append([key, 0])
            self.dma_slots[q] = sl
            self.dma_rr[q] = 0
        self.out_waits = []

    def _deps(self, eng, reads, writes):
        deps = []
        for r in reads:
            if r in self.last_w:
                deps.append(self.last_w[r])
        for w in writes:
            if w in self.last_w:
                deps.append(self.last_w[w])
            deps.extend(self.readers.get(w, {}).items())
        wd = self.waited[eng]
        own = ("c", eng)
        m = {}
        for (key, val) in deps:
            if key == own and (eng == "pe" or not SAME_ENGINE_SYNC):
                continue
            if wd.get(key, 0) >= val:
                continue
            m[key] = max(m.get(key, 0), val)
        for k, v in m.items():
            wd[k] = v
        return list(m.items())

    def _commit(self, prod, reads, writes):
        for r in reads:
            d = self.readers.setdefault(r, {})
            d[prod[0]] = max(d.get(prod[0], 0), prod[1])
        for w in writes:
            self.last_w[w] = prod
            self.readers[w] = {}

    def op(self, eng, fn, reads=(), writes=(), inc=True):
        waits = self._deps(eng, reads, writes)
        if inc:
            self.cnt[eng] += 1
            prod = (("c", eng), self.cnt[eng])
            self.ops[eng].append((waits, fn, prod[0], 1))
        else:
            prod = (("c", eng), self.cnt[eng] + 1)
            self.ops[eng].append((waits, fn, prod[0], 0))
        self._commit(prod, reads, writes)
        return prod

    def dma(self, q, fn, reads=(), writes=(), is_output=False):
        sl = self.dma_slots[q]
        i = self.dma_rr[q]
        self.dma_rr[q] = (i + 1) % len(sl)
        key, tot = sl[i]
        waits = self._deps(q, reads, writes)
        wd = self.waited[q]
        if tot > 0 and wd.get(key, 0) < tot:
            wd[key] = tot
            waits = [(k, v) for (k, v) in waits if k != key] + [(key, tot)]
        sl[i][1] = tot + 16
        prod = (key, tot + 16)
        self.ops[q].append((waits, fn, key, 16))
        self._commit(prod, reads, writes)
        if is_output:
            self.out_waits.append(prod)
        return prod

    def emit(self):
        nc = self.nc
        m = {}
        for k, v in self.out_waits:
            m[k] = max(m.get(k, 0), v)
        final_waits = list(m.items())
        with nc.Block() as block:
            def run(engname, eng):
                for (waits, fn, key, inc) in self.ops[engname]:
                    for (k, v) in waits:
                        eng.wait_ge(self.semobj[k], v)
                    inst = fn(eng)
                    if inc:
                        inst.then_inc(self.semobj[key], inc)
                if engname == "sp":
                    for (k, v) in final_waits:
                        eng.wait_ge(self.semobj[k], v)

            @block.tensor
            def _(e):
                run("pe", e)

            @block.scalar
            def _(e):
                run("act", e)

            @block.vector
            def _(e):
                run("dve", e)

            @block.gpsimd
            def _(e):
                run("pool", e)

            @block.sync
            def _(e):
                run("sp", e)


class Rot:
    def __init__(self, items):
        self.items = items
        self.i = 0

    def next(self):
        it = self.items[self.i]
        self.i = (self.i + 1) % len(self.items)
        return it


def MM(out, lhsT, rhs, start, stop):
    return lambda e: e.matmul(out, lhsT=lhsT, rhs=rhs, start=start, stop=stop, skip_group_check=True)


def TR(out, in_, ident):
    return lambda e: e.transpose(out=out, in_=in_, identity=ident)


def ACT(out, in_, func, scale=None, accum_out=None, bias=None):
    kw = {}
    if bias is not None:
        kw["bias"] = bias
    if scale is not None:
        kw["scale"] = scale
    if accum_out is not None:
        kw["accum_out"] = accum_out
    return lambda e: e.activation(out=out, in_=in_, func=func, **kw)


def TT(out, in0, in1, op):
    return lambda e: e.tensor_tensor(out=out, in0=in0, in1=in1, op=op)


def TS(out, in0, s1, s2, op0, op1):
    return lambda e: e.tensor_scalar(out=out, in0=in0, scalar1=s1, scalar2=s2, op0=op0, op1=op1)


def STT(out, in0, scalar, in1, op0, op1):
    return lambda e: e.scalar_tensor_tensor(out=out, in0=in0, scalar=scalar, in1=in1, op0=op0, op1=op1)


def RED(out, in_, op):
    return lambda e: e.tensor_reduce(out=out, in_=in_, axis=AX.X, op=op)


def CP(out, in_):
    return lambda e: e.tensor_copy(out=out, in_=in_)


def MS(ap, v):
    return lambda e: e.memset(ap, v)


def RCP(out, in_):
    return lambda e: e.reciprocal(out=out, in_=in_)


def DMA(out, in_, **kw):
    return lambda e: e.dma_start(out=out, in_=in_, **kw)


SEGS = [
    ("a_q", 0, 384, "qk", dict(g=0, dst="Q", p0=0)),
    ("a_k", 384, 384, "qk", dict(g=1, dst="K", p0=0)),
    ("b_k", 1920, 384, "qk", dict(g=3, dst="K", p0=3)),
    ("m_q", 3072, 256, "qn", dict(g=4, dst="Q", p0=6)),
    ("b_q", 1536, 384, "qk", dict(g=2, dst="Q", p0=3)),
    ("a_v", 768, 384, "v", dict(h0=0)),
    ("b_v", 2304, 384, "v", dict(h0=6)),
    ("a_g", 1152, 384, "g", dict(h0=0)),
    ("b_g", 2688, 384, "g", dict(h0=6)),
    ("m_g", 3328, 256, "g", dict(h0=12)),
]


class _Stop(Exception):
    pass


STOP_AT = None


def _chk(tag):
    if STOP_AT is not None and tag == STOP_AT:
        raise _Stop()


def build(n_layers=2):
    nc = bass.Bass("TRN2", target_bir_lowering=False)

    def din(name, shape):
        return nc.dram_tensor(name, shape, F32, kind="ExternalInput").ap()

    x_d = din("x", [T, D])
    mem_d = din("mem", [NMEM, D])
    ng_d = din("norm_g", [n_layers, D])
    win_d = din("w_in", [n_layers, D, DIN])
    wout_d = din("w_out", [n_layers, D, D])
    mng_d = din("mem_norm_g", [n_layers, D])
    wmem_d = din("w_mem_kv", [n_layers, D, 512])
    qkg_d = din("qk_g", [n_layers, 384])
    cid_d = din("c_ident", [128, 128])
    cmd_d = din("c_mdil", [128, 1152])
    ctr_d = din("c_tri", [128, 128])
    ceb_d = din("c_eblk", [128, 16])
    ccos_d = din("c_cos", [T, 64])
    csin_d = din("c_sin", [T, 64])
    out_d = nc.dram_tensor("out", [T, D], F32, kind="ExternalOutput").ap()
    xmid_d = [nc.dram_tensor("xmid%d" % i, [T, D], F32, kind="Internal").ap() for i in range(n_layers - 1)]

    with ExitStack() as es:
        S = Sched(nc, es)

        def SB(name, shape, dt):
            return es.enter_context(nc.sbuf_tensor(name, shape, dt))

        Win = SB("Win", [128, 8, DIN], BF16)
        Wout = SB("Wout", [128, 8, D], BF16)
        yT = SB("yT", [128, 8, 512], BF16)
        hT = SB("hT", [128, 8, 512], BF16)
        KT = SB("KT", [128, 6, T], BF16)
        mKTs = [SB("mKT%d" % l, [128, 4, NMEM], BF16) for l in range(n_layers)]
        mkg = SB("mkg", [128, n_layers, 64], F32)
        Vc = SB("Vc", [128, NT, 12, 65], BF16)
        mVs = [SB("mV%d" % l, [128, 2, 4, 65], BF16) for l in range(n_layers)]
        ident = SB("ident", [128, 128], BF16)
        mdil = SB("mdil", [128, 1152], BF16)
        tri = SB("tri", [128, 128], BF16)
        eblk = SB("eblk", [128, 2, 8], BF16)
        cosb = SB("cosb", [128, NT, 64], F32)
        sinb = SB("sinb", [128, NT, 64], F32)
        gbuf = SB("gbuf", [128, D], F32)
        qkg = SB("qkg", [128, 6, 64], F32)
        mhalf = SB("mhalf", [128, 8], F32)
        xa = [SB("xa%d" % i, [128, D], F32) for i in range(2)]
        hbs = [SB("hb%d" % i, [128, D], BF16) for i in range(2)]
        QT = SB("QT", [128, 8, 512], BF16)
        Gt = SB("Gt", [128, 4, D], BF16)
        ptall = SB("ptall", [128, 8, 512], BF16)
        pts = [ptall[:, i, :] for i in range(8)]
        junk = ptall[:, 0:2, :].rearrange("p a c -> p (a c)")
        scrA_all = SB("scrA_all", [128, 2, 384], F32)
        scrA = [scrA_all[:, i, :] for i in range(2)]
        junk2 = scrA_all[:].rearrange("p a c -> p (a c)").bitcast(BF16)[:, 0:D]
        scrB = [SB("scrB%d" % i, [128, 384], F32) for i in range(2)]
        epsb = SB("epsb", [128, 1], F32)
        qktm = [SB("qktm%d" % i, [128, 384], BF16) for i in range(2)]
        biasT = SB("biasT", [128, 3, 512], BF16)
        kmT = SB("kmT", [128, 3, 8], BF16)
        ksum = SB("ksum", [128, 4], F32)
        gsb = SB("gsb", [128, 96], F32)
        rank = SB("rank", [128, 96], F32)
        bqs = [SB("bq%d" % i, [128, 96], BF16) for i in range(2)]
        st = SB("st", [128, 64], F32)
        rc = SB("rc", [128, 4], F32)
        print('SBUF free bytes/partition after alloc:', nc.sbuf_bytes_remaining)
        ps = [es.enter_context(nc.psum_tensor("ps%d" % i, [128, 512], F32)) for i in range(8)]

        pjr = Rot([("ps0", ps[0]), ("ps1", ps[1])])
        tpr = Rot([("ps2", ps[2]), ("ps3", ps[3])])
        str_ = Rot([("ps0", ps[0]), ("ps1", ps[1]), ("ps4", ps[4]), ("ps5", ps[5])])
        oar = Rot([("ps6", ps[6]), ("ps7", ps[7])])
        xar = Rot([(["xa0", "xC0", "xT0", "xT0a", "xT0b", "xH0a", "xH0b"], xa[0]), (["xa1", "xC1", "xT1", "xT1a", "xT1b", "xH1a", "xH1b"], xa[1])])
        ptr = Rot([("pt%d" % i, pts[i]) for i in range(8)])
        qkr = Rot([("qktm%d" % i, qktm[i]) for i in range(2)])
        scr = Rot([0, 1])
        bqr = Rot([("bq%d" % i, bqs[i]) for i in range(2)])
        scr3 = Rot([0, 1, 2])
        nsr = Rot([0, 1])
        hbr = Rot([("hb%d" % i, hbs[i]) for i in range(2)])

        S.dma("pool", DMA(ident[:], cid_d), writes=["ident"])
        S.dma("pool", DMA(mdil[:], cmd_d), writes=["mdil"])
        S.dma("pool", DMA(tri[:], ctr_d), writes=["tri"])
        S.dma("pool", DMA(eblk[:], ceb_d.rearrange("n (t a) -> n t a", t=2)), writes=["eblk"])
        S.dma("sp", DMA(cosb[:], ccos_d.rearrange("(t p) c -> p t c", p=128)), writes=["cos"])
        S.dma("sp", DMA(sinb[:], csin_d.rearrange("(t p) c -> p t c", p=128)), writes=["sin"])
        S.op("pool", MS(mhalf[:], -0.5), writes=["mhalf"])
        S.op("pool", MS(epsb[:], EPS), writes=["epsb"])
        S.op("pool", MS(QT[:], 0.0), writes=[("QT", h_, i_) for h_ in range(8) for i_ in range(4)])
        for l in range(n_layers):
            S.op("pool", MS(mKTs[l][:], 0.0), writes=[("mKT", l, 0), ("mKT", l, 1)])
            S.op("pool", MS(mVs[l][:, :, :, 64:65], 1.0), writes=[("mVones", l)])
            S.dma("sp", DMA(mkg[:, l, :], qkg_d[l:l + 1, 320:384].partition_broadcast(128)), writes=[("mkg", l)])
        S.op("pool", MS(kmT[:], 0.0), writes=[("kmT", n) for n in range(8)])
        S.op("pool", MS(biasT[:], 0.0), writes=[("biasT", i) for i in range(4)])
        S.op("pool", MS(Vc[:, :, :, 64:65], 1.0), writes=["Vones"])

        def h3(ap, nh):
            return ap.rearrange("p (h d) -> p h d", d=64) if len(ap.shape) == 2 else ap

        def norm_stages(src_ap, src_res, col0, ht_res, alt_junk=False):
            hold = {}

            def s0():
                xres, xt = xar.next()
                hres, hbt = hbr.next()
                sp = nsr.next()
                hold.update(x=(xres, xt), h=(hres, hbt), sp=sp)
                o = 56 + sp * 4
                S.dma("sp", DMA(xt[:], src_ap), reads=src_res, writes=xres)
                if alt_junk:
                    S.op("act", ACT(junk2, xt[:], AF.Square, accum_out=st[:, o:o + 1]), reads=xres, writes=["scrA0", "scrA1", "nss%d" % sp])
                else:
                    S.op("act", ACT(junk, xt[:], AF.Square, accum_out=st[:, o:o + 1]), reads=xres, writes=["pt0", "pt1", "nss%d" % sp])
                S.op("act", ACT(st[:, o + 1:o + 2], st[:, o:o + 1], AF.Ln, scale=1.0 / D, bias=epsb[:, 0:1]), reads=["nss%d" % sp, "epsb"], writes=["nlv%d" % sp])
                S.op("act", ACT(st[:, o + 2:o + 3], st[:, o + 1:o + 2], AF.Exp, scale=-0.5), reads=["nlv%d" % sp], writes=["nrs%d" % sp])

            def s1():
                xres, xt = hold["x"]
                hres, hbt = hold["h"]
                o = 56 + hold["sp"] * 4
                S.op("dve", STT(hbt[:], xt[:], st[:, o + 2:o + 3], gbuf[:], ALU.mult, ALU.mult), reads=xres + ["nrs%d" % hold["sp"], "gbuf"], writes=[hres])

            def s2():
                hres, hbt = hold["h"]
                tres, tp = tpr.next()
                tpb = tp[:].bitcast(BF16)
                for k in range(8):
                    S.op("pe", TR(tpb[:, k * 128:(k + 1) * 128], hbt[:, k * 128:(k + 1) * 128], ident[:]), reads=[hres, "ident"], writes=[tres], inc=(k == 7))
                S.op("act", ACT(hT[:, :, col0:col0 + 128], tpb.rearrange("p (k c) -> p k c", c=128), AF.Copy), reads=[tres], writes=ht_res)

            return (s0, s1, s2, None)

        def norm_parts(src_ap, src_res, col0, ht_res):
            hold = {}

            def dma():
                xres, xt = xar.next()
                hold["x"] = (xres, xt)
                S.dma("sp", DMA(xt[:], src_ap), reads=src_res, writes=xres)

            def act():
                xres, xt = hold["x"]
                hres, hbt = hbr.next()
                sp = nsr.next()
                hold.update(h=(hres, hbt), sp=sp)
                o = 56 + sp * 4
                S.op("dve", (lambda o_, x_, a_: (lambda e: e.scalar_tensor_tensor(out=o_, in0=x_, scalar=1.0, in1=x_, op0=ALU.mult, op1=ALU.mult, accum_out=a_)))(junk2, xt[:], st[:, o:o + 1]),
                     reads=xres, writes=["scrA0", "scrA1", "nss%d" % sp])
                S.op("act", ACT(st[:, o + 1:o + 2], st[:, o:o + 1], AF.Ln, scale=1.0 / D, bias=epsb[:, 0:1]), reads=["nss%d" % sp, "epsb"], writes=["nlv%d" % sp])
                S.op("act", ACT(st[:, o + 2:o + 3], st[:, o + 1:o + 2], AF.Exp, scale=-0.5), reads=["nlv%d" % sp], writes=["nrs%d" % sp])

            def dve():
                xres, xt = hold["x"]
                hres, hbt = hold["h"]
                o = 56 + hold["sp"] * 4
                S.op("dve", STT(hbt[:], xt[:], st[:, o + 2:o + 3], gbuf[:], ALU.mult, ALU.mult), reads=xres + ["nrs%d" % hold["sp"], "gbuf"], writes=[hres])

            def tr():
                hres, hbt = hold["h"]
                tres, tp = tpr.next()
                tpb = tp[:].bitcast(BF16)
                for k in range(8):
                    S.op("pe", TR(tpb[:, k * 128:(k + 1) * 128], hbt[:, k * 128:(k + 1) * 128], ident[:]), reads=[hres, "ident"], writes=[tres], inc=(k == 7))
                S.op("dve", CP(hT[:, :, col0:col0 + 128], tpb.rearrange("p (k c) -> p k c", c=128)), reads=[tres], writes=ht_res)

            return dict(dma=dma, act=act, dve=dve, tr=tr)

        def norm_tile(src_ap, src_res, col0, ht_res):
            a, b, c, _ = norm_stages(src_ap, src_res, col0, ht_res)
            a()
            b()
            c()

        def evac_qk_slots(ps_ap, ps_res, nh, gidx, rope_tile, dest_ap, dest_res):
            w = nh * 64
            sp = scr.next()
            A, B = scrA[sp], scrB[sp]
            rA, rB = "scrA%d" % sp, "scrB%d" % sp
            o = 8 + sp * 24
            ss, lv, rs = st[:, o:o + nh], st[:, o + 8:o + 8 + nh], st[:, o + 16:o + 16 + nh]
            rss, rlv, rrs = "ss%d" % sp, "lv%d" % sp, "rs%d" % sp
            ps3 = ps_ap.rearrange("p (h d) -> p h d", d=64)
            A3 = A[:, 0:w].rearrange("p (h d) -> p h d", d=64)
            B3 = B[:, 0:w].rearrange("p (h d) -> p h d", d=64)
            hold = {}
            Cb = xa[sp][:, 0:w].rearrange("p (h d) -> p h d", d=64)
            TA = xa[sp][:, 512:576]
            TB = xa[sp][:, 576:640]
            H3 = xa[sp][:, 640:640 + w].rearrange("p (h d) -> p h d", d=64)
            rC, rT, rH = "xC%d" % sp, "xT%d" % sp, "xH%d" % sp
            if isinstance(gidx, tuple):
                gv, gres = mkg[:, gidx[1], :], ("mkg", gidx[1])
            else:
                gv, gres = qkg[:, gidx, :], "qkg"
            rope = rope_tile is not None
            sl = {}

            def act_sq():
                S.op("act", ACT(A[:, 0:w], ps_ap, AF.Square), reads=[ps_res], writes=[rA])
            sl["act_sq"] = act_sq

            def dve_red():
                S.op("dve", RED(ss, A3, ALU.add), reads=[rA], writes=[rss])
            sl["dve_red"] = dve_red

            def act_lnexp():
                S.op("act", ACT(lv, ss, AF.Ln, scale=1.0 / 64, bias=epsb[:, 0:1]), reads=[rss, "epsb"], writes=[rlv])
                S.op("act", ACT(rs, lv, AF.Exp, scale=-0.5), reads=[rlv], writes=[rrs])
            sl["act_lnexp"] = act_lnexp

            def dve_tab():
                S.op("pool", TT(TA, cosb[:, rope_tile, :], gv, ALU.mult), reads=["cos", gres], writes=[rT])
                S.op("pool", TT(TB[:, 0:32], sinb[:, rope_tile, 0:32], gv[:, 32:64], ALU.mult), reads=["sin", gres], writes=[rT + "a"])
                S.op("pool", TT(TB[:, 32:64], sinb[:, rope_tile, 32:64], gv[:, 0:32], ALU.mult), reads=["sin", gres], writes=[rT + "b"])
            if rope:
                sl["dve_tab"] = dve_tab

            def dve_BC():
                S.op("dve", TT(B3, ps3, rs.unsqueeze(2).broadcast_to([128, nh, 64]), ALU.mult), reads=[ps_res, rrs], writes=[rB])
                qres, qb = qkr.next()
                hold["q"] = (qres, qb)
                qb3 = qb[:, 0:w].rearrange("p (h d) -> p h d", d=64)
                if not rope:
                    pass
                else:
                    S.op("dve", TT(Cb, B3, TA.unsqueeze(1).broadcast_to([128, nh, 64]), ALU.mult), reads=[rB, rT], writes=[rC])
            sl["dve_BC"] = dve_BC

            def pool_halves():
                H4 = xa[sp][:, 640:640 + w].rearrange("p (h j d) -> p h j d", j=2, d=32)
                Br = B[:, 0:w].rearrange("p (h j d) -> p h j d", j=2, d=32)[:, :, ::-1, :]
                TB4 = TB.rearrange("p (j d) -> p j d", d=32).unsqueeze(1).broadcast_to([128, nh, 2, 32])
                S.op("pool", TT(H4, Br, TB4, ALU.mult), reads=[rB, rT + "a", rT + "b"], writes=[rH + "a", rH + "b"])
            if rope:
                sl["pool_halves"] = pool_halves

            def dve_add():
                qres, qb = hold["q"]
                qb3 = qb[:, 0:w].rearrange("p (h d) -> p h d", d=64)
                if rope:
                    S.op("dve", TT(qb3, Cb, H3, ALU.add), reads=[rC, rH + "a", rH + "b"], writes=[qres])
                else:
                    S.op("dve", TT(qb3, B3, gv.unsqueeze(1).broadcast_to([128, nh, 64]), ALU.mult), reads=[rB, gres], writes=[qres])
            sl["dve_add"] = dve_add

            def pe_tr():
                qres, qb = hold["q"]
                tres, tp = tpr.next()
                tpb = tp[:].bitcast(BF16)
                hold["tp"] = (tres, tpb)
                for pi in range(nh // 2):
                    S.op("pe", TR(tpb[:, pi * 128:(pi + 1) * 128], qb[:, pi * 128:(pi + 1) * 128], ident[:]), reads=[qres, "ident"], writes=[tres], inc=(pi == nh // 2 - 1))
            sl["pe_tr"] = pe_tr

            def act_copy():
                tres, tpb = hold["tp"]
                if callable(dest_ap):
                    dest_ap(tpb, tres)
                else:
                    S.op("act", ACT(dest_ap, tpb[:, 0:(nh // 2) * 128].rearrange("p (a c) -> p a c", c=128), AF.Copy), reads=[tres], writes=dest_res)
            sl["act_copy"] = act_copy
            return sl

        SLOT_SCHED = [("dve_add", 6), ("act_sq", 1), ("pe_tr", 7), ("dve_red", 2), ("act_lnexp", 3),
                      ("dve_BC", 4), ("pool_halves", 4), ("dve_tab", 3), ("act_copy", 8), ("pe_mm", 0)]
        SLOT_DEPTH = 8
        SLOT_LOGICAL = ["pe_mm", "act_sq", "dve_red", "act_lnexp", "dve_tab", "dve_BC", "pool_halves", "dve_add", "pe_tr", "act_copy"]

        def run_slots(units, extras=None):
            n = len(units)
            extras = extras or {}
            if not (SKEW_ON & 2):
                for u in units:
                    for nm in SLOT_LOGICAL:
                        if nm in u:
                            u[nm]()
                for t in sorted(extras):
                    for f in extras[t]:
                        f()
                return
            for t in range(max(n + SLOT_DEPTH, max(extras) + 1 if extras else 0)):
                for (nm, k) in SLOT_SCHED:
                    u = t - k
                    if 0 <= u < n and nm in units[u]:
                        units[u][nm]()
                for f in extras.get(t, []):
                    f()

        def evac_qk(*a):
            sl = evac_qk_slots(*a)
            for nm in SLOT_LOGICAL:
                if nm in sl:
                    sl[nm]()

        def run_skewed(units, skew=1):
            n = len(units)
            ns = max(len(u) for u in units)
            if not (SKEW_ON & skew):
                for u in units:
                    for f in u:
                        if f is not None:
                            f()
                return
            for t in range(n + ns - 1):
                for k in range(ns - 1, -1, -1):
                    u = t - k
                    if 0 <= u < n and k < len(units[u]) and units[u][k] is not None:
                        units[u][k]()

        pj4 = Rot([("ps0", ps[0]), ("ps1", ps[1]), ("ps4", ps[4]), ("ps5", ps[5]), ("ps6", ps[6])])

        def make_unit(G, il, c0, w, kind, ex, bankrot=None):
            tile = 4 * G + il
            hold = {}
            u = {}

            def pe_mm():
                pres, pj = (bankrot or pj4).next()
                hold["pj"] = (pres, pj)
                for k in range(8):
                    S.op("pe", MM(pj[:, 0:w], hT[:, k, il * 128:(il + 1) * 128], Win[:, k, c0:c0 + w], k == 0, k == 7),
                         reads=[("hT", il), ("Win", c0)], writes=[pres], inc=(k == 7))
            u["pe_mm"] = pe_mm

            if kind in ("qk", "qn"):
                nh = w // 64
                npair = nh // 2
                p0 = ex["p0"]
                if ex["dst"] == "Q" and kind == "qk":
                    base = QT if p0 == 0 else yT
                    zero_too = (p0 != 0)
                    dres = [("QT", i, il) for i in range(6)] if p0 == 0 else [("yT", il)]
                    bv = base[:, 0:6, il * 128:(il + 1) * 128].rearrange("p (a two) c -> p a two c", two=2)

                    def dest(tpb, tres, bv=bv, dres=dres, zero_too=zero_too):
                        src = tpb[:, 0:384].rearrange("p (a c) -> p a c", c=128)
                        for par in range(2):
                            rows = slice(par * 64, par * 64 + 64)
                            S.op("act", ACT(bv[rows, :, par, :], src[rows], AF.Copy), reads=[tres], writes=dres)
                            if zero_too:
                                orow = slice((1 - par) * 64, (1 - par) * 64 + 64)
                                S.op("act", ACT(bv[orow, :, par, :], src[orow], AF.Copy, scale=0.0), reads=[tres], writes=dres)
                elif ex["dst"] == "Q":
                    dest = QT[:, p0:p0 + npair, il * 128:(il + 1) * 128]
                    dres = [("QT", p0 + i, il) for i in range(npair)]
                else:
                    dest = KT[:, p0:p0 + npair, tile * 128:(tile + 1) * 128]
                    dres = [("KT", p0 + i, tile) for i in range(npair)]

                def lazy(nm):
                    def f():
                        if "sl" not in hold:
                            pres, pj = hold["pj"]
                            hold["sl"] = evac_qk_slots(pj[:, 0:w], pres, nh, ex["g"], tile if kind == "qk" else None, dest, dres)
                        if nm in hold["sl"]:
                            hold["sl"][nm]()
                    return f
                for nm in SLOT_LOGICAL[1:]:
                    if kind == "qn" and nm in ("dve_tab", "pool_halves"):
                        continue
                    u[nm] = lazy(nm)
            elif kind == "v":
                h0 = ex["h0"]

                def vcopy():
                    pres, pj = hold["pj"]
                    S.op("dve", CP(Vc[:, tile, h0:h0 + 6, 0:64], pj[:, 0:w].rearrange("p (h d) -> p h d", d=64)),
                         reads=[pres, "Vones"], writes=[("V", tile, h0 // 6)])
                u["dve_red"] = vcopy
            else:
                h0 = ex["h0"]

                def gsilu():
                    pres, pj = hold["pj"]
                    S.op("act", ACT(Gt[:, il, h0 * 64:h0 * 64 + w], pj[:, 0:w], AF.Silu), reads=[pres],
                         writes=[("Gt", il, h0 + i) for i in range(w // 64)])
                u["act_sq"] = gsilu
            return u

        for L in range(n_layers):
            S.dma("pool", DMA(yT[:], wmem_d[L].rearrange("(k p) c -> p k c", p=128)), writes=[("yT", i) for i in range(4)])
            if L == n_layers - 1:
                for (sname, c0, w, kind, ex) in SEGS:
                    S.dma("pool", DMA(Win[:, :, c0:c0 + w], win_d[0][:, c0:c0 + w].rearrange("(k p) c -> p k c", p=128)), writes=[("Win", c0)])
            S.dma("sp", DMA(gbuf[:], mng_d[L:L + 1, :].partition_broadcast(128)), writes=["gbuf"])
            def mem_unit(mt, L=L):
                n0, n1, n2, _ = norm_stages(mem_d[mt * 128:(mt + 1) * 128, :], [], mt * 128, [("hT", mt)])
                hold = {}

                def mm():
                    pres, pj = pjr.next()
                    hold["pj"] = (pres, pj)
                    for k in range(8):
                        S.op("pe", MM(pj[:, 0:512], hT[:, k, mt * 128:(mt + 1) * 128], yT[:, k, :], k == 0, k == 7),
                             reads=[("hT", mt)] + [("yT", i) for i in range(4)], writes=[pres], inc=(k == 7))

                mkv = mKTs[L][:, :, mt * 128:(mt + 1) * 128].rearrange("p (a two) c -> p a two c", two=2)

                def mdest(tpb, tres):
                    src = tpb[:, 0:256].rearrange("p (a c) -> p a c", c=128)
                    for par in range(2):
                        rows = slice(par * 64, par * 64 + 64)
                        S.op("act", ACT(mkv[rows, :, par, :], src[rows], AF.Copy), reads=[tres], writes=[("mKT", L, mt)])

                def ev(names, extra=None):
                    def f():
                        if "sl" not in hold:
                            pres, pj = hold["pj"]
                            hold["sl"] = evac_qk_slots(pj[:, 0:256], pres, 4, ("mkg", L), None, mdest, [("mKT", L, mt)])
                        for nm in names:
                            if nm in hold["sl"]:
                                hold["sl"][nm]()
                        if extra:
                            extra()
                    return f

                def vcopy():
                    pres, pj = hold["pj"]
                    S.op("dve", CP(mVs[L][:, mt, :, 0:64], pj[:, 256:512].rearrange("p (h d) -> p h d", d=64)), reads=[pres, ("mVones", L)], writes=[("mV", L, mt)])

                return (n0, n1, n2, mm, ev(["act_sq", "dve_red"], vcopy), ev(["act_lnexp"]), ev(["dve_BC"]), ev(["dve_add"]), ev(["pe_tr"]), ev(["act_copy"]))

            run_skewed([mem_unit(0), mem_unit(1)])

        def make_fg_phases(Gp, src_p, src_id_p, dst_p, dst_id_p, last_p):
            phases = []
            tpb7 = ps[7][:].bitcast(BF16)

            def xload(il, half):
                t_ = 4 * Gp + il
                S.dma("sp", DMA(hbs[half][:].bitcast(F32), src_p[t_ * 128:(t_ + 1) * 128, half * 512:(half + 1) * 512]),
                      reads=[("xd", src_id_p, t_)], writes=["hb%d" % half])

            for il in range(4):
                tile = 4 * Gp + il

                def p_tr(il=il):
                    if il == 0:
                        xload(0, 0)
                        xload(0, 1)
                    for k in range(8):
                        S.op("pe", TR(tpb7[:, k * 128:(k + 1) * 128], Gt[:, il, k * 128:(k + 1) * 128], ident[:]),
                             reads=[("Gt", il, 2 * k), ("Gt", il, 2 * k + 1), "ident"], writes=["ps7"], inc=(k == 7))

                def p_cp(il=il):
                    S.op("act", ACT(yT[:, :, il * 128:(il + 1) * 128], tpb7.rearrange("p (k c) -> p k c", c=128), AF.Copy), reads=["ps7"], writes=[("yT", il)])

                def p_mm(half, il=il):
                    for k in range(8):
                        S.op("pe", MM(ps[7][:, 0:512], yT[:, k, il * 128:(il + 1) * 128], Wout[:, k, half * 512:(half + 1) * 512], k == 0, k == 7),
                             reads=[("yT", il), ("Wout", k)], writes=["ps7"], inc=(k == 7))

                def p_add(half, il=il, tile=tile):
                    xt = hbs[half][:].bitcast(F32)
                    S.op("dve", TT(xt, ps[7][:, 0:512], xt, ALU.add), reads=["ps7", "hb%d" % half], writes=["hb%d" % half])
                    S.dma("sp", DMA(dst_p[tile * 128:(tile + 1) * 128, half * 512:(half + 1) * 512], xt), reads=["hb%d" % half],
                          writes=[("xd", dst_id_p, tile)], is_output=last_p)
                    if il + 1 < 4:
                        xload(il + 1, half)

                phases += [p_tr, p_cp, (lambda il=il: p_mm(0, il)), (lambda il=il, tile=tile: p_add(0, il, tile)),
                           (lambda il=il: p_mm(1, il)), (lambda il=il, tile=tile: p_add(1, il, tile))]
            return phases

        pending_fg = []
        pre_a = {"done": False}
        pre_v = {"done": False}
        ps1r = Rot([("ps1", ps[1])])
        try:
          for L in range(n_layers):
              src_d = x_d if L == 0 else xmid_d[L - 1]
              src_id = "x" if L == 0 else "mid%d" % (L - 1)
              last = (L == n_layers - 1)
              dst_d = out_d if last else xmid_d[L]
              dst_id = "out" if last else "mid%d" % L

              if L > 0:
                  for (sname, c0, w, kind, ex) in SEGS:
                      S.dma("pool", DMA(Win[:, :, c0:c0 + w], win_d[L][:, c0:c0 + w].rearrange("(k p) c -> p k c", p=128)), writes=[("Win", c0)])
              if not pre_a["done"]:
                  S.dma("sp", DMA(gbuf[:], ng_d[L:L + 1, :].partition_broadcast(128)), writes=["gbuf"])
              S.dma("sp", DMA(qkg[:].rearrange("p a d -> p (a d)"), qkg_d[L:L + 1, :].partition_broadcast(128)), writes=["qkg"])
              mKT, mV = mKTs[L], mVs[L]

              _chk('mem')
              for G in range(NG):
                  if not pre_a["done"]:
                      run_skewed([norm_stages(src_d[(4 * G + il) * 128:(4 * G + il + 1) * 128, :], [("xd", src_id, 4 * G + il)], il * 128, [("hT", il)])
                                  for il in range(4)])
                  pre_a["done"] = False
                  _chk('a%d' % G)
                  uq, uv, ug = [], [], []
                  for (sname, c0, w, kind, ex) in SEGS:
                      if kind == "v" and pre_v["done"]:
                          continue
                      for il in range(4):
                          u = make_unit(G, il, c0, w, kind, ex)
                          (uq if kind in ("qk", "qn") else uv if kind == "v" else ug).append(u)
                  pre_v["done"] = False
                  units = []
                  for i, u in enumerate(uq):
                      units.append(u)
                      if INTERLEAVE_V and i % 2 == 1 and uv:
                          units.append(uv.pop(0))
                  units += uv + ug

                  def kmean_ops():
                      for n in (2 * G, 2 * G + 1):
                          S.op("dve", RED(ksum[:, 0:3], KT[:, 0:3, n * 256:(n + 1) * 256], ALU.add),
                               reads=[("KT", p, t) for p in range(3) for t in (2 * n, 2 * n + 1)], writes=["ksum"])
                          S.op("dve",
                               (lambda o, i: (lambda e: e.tensor_scalar_mul(out=o, in0=i, scalar1=1.0 / 256)))(kmT[:, :, n], ksum[:, 0:3]),
                               reads=["ksum"], writes=[("kmT", n)])
                  extras = {25: [kmean_ops]}
                  for i_, f_ in enumerate(pending_fg):
                      extras.setdefault(1 + i_, []).append(f_)
                  del pending_fg[:]
                  gated_group = (2 * G + 1) > 3
                  if gated_group:
                      def gate_unit(b):
                          c = (4 * G + 2 * b) // 2
                          hold = {}
                          g4 = gsb[:].rearrange("p (t s n) -> p t s n", t=2, n=8)
                          g3 = gsb[:].rearrange("p (a n) -> p a n", n=8)

                          def s0():
                              for ti in range(2):
                                  il = 2 * b + ti
                                  for h in range(6):
                                      col = ti * 48 + h * 8
                                      S.op("pe", MM(ps[7][:, col:col + 8], QT[:, h, il * 128:(il + 1) * 128], kmT[:, h // 2, :], True, True),
                                           reads=[("QT", h, il)] + [("kmT", n) for n in range(8)], writes=["ps7"])

                          def s1():
                              hold["bq"] = bqr.next()
                              bres_, bq = hold["bq"]
                              S.op("dve", CP(gsb[:], ps[7][:, 0:96]), reads=["ps7"], writes=["gsb"])
                              S.op("dve", MS(g3[:, :, c:8], -1e30), reads=[], writes=["gsb"])
                              for ti in range(2):
                                  cmf = hbs[ti][:].bitcast(F32)[:, 0:384]
                                  gt = g3[:, ti * 6:(ti + 1) * 6, :]
                                  S.op("dve", TT(cmf.rearrange("p (a n m) -> p a n m", n=8, m=8), gt.unsqueeze(2).broadcast_to([128, 6, 8, 8]),
                                                 gt.unsqueeze(3).broadcast_to([128, 6, 8, 8]), ALU.is_gt), reads=["gsb"], writes=["hb%d" % ti])
                                  S.op("dve", RED(rank[:, ti * 48:(ti + 1) * 48], cmf.rearrange("p (a m) -> p a m", m=8), ALU.add),
                                       reads=["hb%d" % ti], writes=["rank"])
                              S.op("dve", TS(bq[:], rank[:], 3.0, -BIG, ALU.is_ge, ALU.mult), reads=["rank"], writes=[bres_])
                              b3 = bq[:].rearrange("p (a n) -> p a n", n=8)
                              S.op("dve", MS(b3[:, :, c:c + 1], 0.0), reads=[], writes=[bres_])

                          def s2():
                              bres_, bq = hold["bq"]
                              b3 = bq[:].rearrange("p (a n) -> p a n", n=8)
                              tpb2 = ps[7][:].bitcast(BF16)
                              for ti in range(2):
                                  for h in range(6):
                                      par, pr = h % 2, h // 2
                                      S.op("pe", TR(tpb2[par * 64:par * 64 + 8, (ti * 3 + pr) * 128:(ti * 3 + pr + 1) * 128], b3[:, ti * 6 + h, :], ident[:]),
                                           reads=[bres_, "ident"], writes=["ps7"])

                          def s3():
                              tpb2 = ps[7][:].bitcast(BF16)
                              for ti in range(2):
                                  il = 2 * b + ti
                                  for par in range(2):
                                      S.op("dve", CP(biasT[par * 64:par * 64 + 8, :, il * 128:(il + 1) * 128],
                                                     tpb2[par * 64:par * 64 + 8, ti * 384:(ti + 1) * 384].rearrange("p (h c) -> p h c", c=128)),
                                           reads=["ps7"], writes=[("biasT", il)])
                          return (s0, s1, s2, s3)

                      g0, g1 = gate_unit(0), gate_unit(1)
                      for i_, f_ in enumerate(list(g0) + list(g1)):
                          extras.setdefault(27 + 2 * i_, []).append(f_)
                  run_slots(units, extras)
                  if G == 0:
                      for k in range(8):
                          S.dma("pool", DMA(Wout[:, k, :], wout_d[L, k * 128:(k + 1) * 128, :]), writes=[("Wout", k)])
                  _chk('d%d' % G)
                  steps = []
                  for h in range(16):
                      if h < 12:
                          kind = "A" if h < 6 else "B"
                          pair = h // 2
                          nkt = 4 * G + 4
                      else:
                          kind = "M"
                          pair = 6 + (h - 12) // 2
                          nkt = 2
                      for j in range(nkt):
                          steps.append(dict(h=h, kind=kind, pair=pair, nkt=nkt, j=j))
                  SKEW = ATT_SKEW
                  cur_o = {}

                  def emit_S(sd):
                      h, kind, pair, j = sd["h"], sd["kind"], sd["pair"], sd["j"]
                      r0 = (h % 2) * 64
                      qs = 512 * G if kind == "M" else max(512 * G, 128 * j)
                      N = 512 * (G + 1) - qs
                      ql = qs - 512 * G
                      sres, stp = str_.next()
                      il_lo = ql // 128
                      if kind == "M":
                          lhsT = mKT[:, h - 12, j * 128:(j + 1) * 128]
                          kreads = [("mKT", L, j)]
                          rhsq = QT[:, pair, ql:ql + N]
                          qreads = [("QT", pair, i) for i in range(il_lo, 4)]
                      elif kind == "A":
                          lhsT = KT[:, pair, j * 128:(j + 1) * 128]
                          kreads = [("KT", pair, j)]
                          rhsq = QT[:, h, ql:ql + N]
                          qreads = [("QT", h, i) for i in range(il_lo, 4)]
                      else:
                          lhsT = KT[:, pair, j * 128:(j + 1) * 128]
                          kreads = [("KT", pair, j)]
                          rhsq = yT[:, h - 6, ql:ql + N]
                          qreads = [("yT", i) for i in range(il_lo, 4)]
                      n = j // 2
                      gated = (kind == "A") and gated_group and (n <= 2 * G)
                      S.op("pe", MM(stp[:, 0:N], lhsT, rhsq, True, not gated), reads=kreads + qreads, writes=[sres], inc=(not gated))
                      if gated:
                          S.op("pe", MM(stp[:, 0:N], eblk[:, h % 2, n:n + 1].broadcast_to([128, 128]), biasT[:, pair, ql:ql + N], False, True),
                               reads=["eblk"] + [("biasT", i) for i in range(il_lo, 4)], writes=[sres])
                      pres_, pt = ptr.next()
                      S.op("act", ACT(pt[:, 0:N], stp[:, 0:N], AF.Exp, scale=0.125), reads=[sres], writes=[pres_])
                      if kind == "A" and j >= 4 * G:
                          S.op("dve", TT(pt[:, 0:128], pt[:, 0:128], tri[:], ALU.mult), reads=[pres_, "tri"], writes=[pres_])
                      if kind == "B":
                          x0 = min(qs - 128 * j, 640)
                          S.op("dve", TT(pt[:, 0:N], pt[:, 0:N], mdil[:, x0:x0 + N], ALU.mult), reads=[pres_, "mdil"], writes=[pres_])
                      sd["pt"] = (pres_, pt)
                      sd["qs"] = qs

                  def emit_PV(sd):
                      h, kind, j, nkt, qs = sd["h"], sd["kind"], sd["j"], sd["nkt"], sd["qs"]
                      pres_, pt = sd["pt"]
                      if j == 0:
                          cur_o["o"] = oar.next()
                      ores, oa = cur_o["o"]
                      for i in range(qs // 128, 4 * G + 4):
                          il = i - 4 * G
                          c0 = i * 128 - qs
                          if kind == "M":
                              rhs = mV[:, j, h - 12, :]
                              vreads = [("mV", L, j), ("mVones", L)]
                          else:
                              rhs = Vc[:, j, h, :]
                              vreads = [("V", j, h // 6), "Vones"]
                          S.op("pe", MM(oa[:, il * 128:il * 128 + 65], pt[:, c0:c0 + 128], rhs, (j == 0 and i == qs // 128), j == nkt - 1),
                               reads=[pres_] + vreads, writes=[ores], inc=(i == 4 * G + 3))
                      if j == nkt - 1:
                          oa3 = oa[:].rearrange("p (a c) -> p a c", c=128)
                          S.op("dve", RCP(rc[:, 0:4], oa3[:, :, 64]), reads=[ores], writes=["rc"])
                          for il in range(4):
                              gap = Gt[:, il, h * 64:(h + 1) * 64]
                              S.op("dve", STT(gap, oa[:, il * 128:il * 128 + 64], rc[:, il:il + 1], gap, ALU.mult, ALU.mult),
                                   reads=[ores, "rc", ("Gt", il, h)], writes=[("Gt", il, h)])

                  vq = []
                  nsched = {}
                  for idx in range(len(steps) + SKEW):
                      if PRE_A and idx == 0 and not (G == NG - 1 and L == n_layers - 1):
                          if G < NG - 1:
                              nsrc, nsid, nG = src_d, src_id, G + 1
                          else:
                              nsrc, nsid, nG = dst_d, dst_id, 0
                              S.dma("sp", DMA(gbuf[:], ng_d[L + 1:L + 2, :].partition_broadcast(128)), writes=["gbuf"])
                          nparts = [norm_parts(nsrc[(4 * nG + il) * 128:(4 * nG + il + 1) * 128, :], [("xd", nsid, 4 * nG + il)], il * 128, [("hT", il)])
                                    for il in range(4)]
                          nsched = {}
                          for il in range(4):
                              base = (il + 1) * (6 * (4 * G + 4)) // 5
                              for off, nm in ((-5, "dma"), (0, "act"), (1, "dve"), (3, "tr")):
                                  nsched.setdefault(max(0, base + off), []).append(nparts[il][nm])
                          pre_a["done"] = True
                          if PRE_V and G < NG - 1 and False:
                              vq = [make_unit(G + 1, il, c0, w, kind, ex, bankrot=ps1r) for (sname, c0, w, kind, ex) in SEGS if kind == "v" for il in range(4)]
                              pre_v["done"] = True
                      for f in nsched.get(idx, []):
                          f()
                      if vq and idx > len(steps) // 4 and (idx - len(steps) // 4) % max(1, (len(steps) * 5 // 8) // 8) == 0:
                          u = vq.pop(0)
                          u["pe_mm"]()
                          u["dve_red"]()
                      if idx < len(steps):
                          emit_S(steps[idx])
                      if idx >= SKEW:
                          emit_PV(steps[idx - SKEW])
                  while vq:
                      u = vq.pop(0)
                      u["pe_mm"]()
                      u["dve_red"]()
                  _chk('e%d' % G)
                  if DEFER_FG and not (G == NG - 1 and L == n_layers - 1):
                      pending_fg.extend(make_fg_phases(G, src_d, src_id, dst_d, dst_id, last))
                      continue
                  xq = []

                  def issue_xload(il_):
                      t_ = 4 * G + il_
                      xr_, xt_ = xar.next()
                      S.dma("sp", DMA(xt_[:], src_d[t_ * 128:(t_ + 1) * 128, :]), reads=[("xd", src_id, t_)], writes=xr_)
                      xq.append((xr_, xt_))
                  issue_xload(0)
                  issue_xload(1)
                  def og_unit(il):
                      tile = 4 * G + il
                      hold = {}

                      def s0():
                          tres, tp = tpr.next()
                          tpb = tp[:].bitcast(BF16)
                          hold["tp"] = (tres, tpb)
                          for k in range(8):
                              S.op("pe", TR(tpb[:, k * 128:(k + 1) * 128], Gt[:, il, k * 128:(k + 1) * 128], ident[:]),
                                   reads=[("Gt", il, 2 * k), ("Gt", il, 2 * k + 1), "ident"], writes=[tres], inc=(k == 7))

                      def s1():
                          tres, tpb = hold["tp"]
                          S.op("dve", CP(yT[:, :, il * 128:(il + 1) * 128], tpb.rearrange("p (k c) -> p k c", c=128)), reads=[tres], writes=[("yT", il)])

                      def s2():
                          hold["pj"] = [pj4.next(), pj4.next()]
                          for half in range(2):
                              pres, pj = hold["pj"][half]
                              for k in range(8):
                                  S.op("pe", MM(pj[:, 0:512], yT[:, k, il * 128:(il + 1) * 128], Wout[:, k, half * 512:(half + 1) * 512], k == 0, k == 7),
                                       reads=[("yT", il), ("Wout", k)], writes=[pres], inc=(k == 7))

                      def s3():
                          xres, xt = xq.pop(0)
                          for half in range(2):
                              pres, pj = hold["pj"][half]
                              S.op("dve", TT(xt[:, half * 512:(half + 1) * 512], pj[:, 0:512], xt[:, half * 512:(half + 1) * 512], ALU.add),
                                   reads=[pres] + xres, writes=xres)
                          S.dma("sp", DMA(dst_d[tile * 128:(tile + 1) * 128, :], xt[:]), reads=xres, writes=[("xd", dst_id, tile)], is_output=last)
                          if il + 2 < 4:
                              issue_xload(il + 2)
                      return (s0, s1, s2, s3)

                  run_skewed([og_unit(il) for il in range(4)])
        except _Stop:
            pass
        S.emit()
    return nc


def make_consts():
    kk = np.arange(128)[:, None]
    xx = np.arange(T)[None, :]
    d = xx - kk
    c = ((d >= 0) & (d <= 128)).astype(np.float32) + ((d >= 0) & (d <= 512) & (d % 4 == 0)).astype(np.float32) \
        + ((d >= 0) & (d % 16 == 0)).astype(np.float32)
    tri = (np.arange(128)[None, :] >= kk).astype(np.float32)
    eblk = np.zeros((128, 2, 8), np.float32)
    for n in range(8):
        eblk[n, 0, n] = 1.0
        eblk[64 + n, 1, n] = 1.0
    half = 32
    inv = np.float32(10000.0) ** (-(np.arange(half, dtype=np.float32) / np.float32(half)))
    ang = (np.arange(T, dtype=np.float32)[:, None] * inv[None, :].astype(np.float32)).astype(np.float32)
    cos = np.cos(ang.astype(np.float64)).astype(np.float32)
    sin = np.sin(ang.astype(np.float64)).astype(np.float32)
    cos2 = np.concatenate([cos, cos], axis=1)
    sin2 = np.concatenate([-sin, sin], axis=1)
    return dict(c_ident=np.eye(128, dtype=np.float32), c_mdil=np.ascontiguousarray(c[:, :1152]), c_tri=tri,
                c_eblk=eblk.reshape(128, 16), c_cos=np.ascontiguousarray(cos2), c_sin=np.ascontiguousarray(sin2))


_NC_CACHE = {}


def _get_nc(n_layers):
    if n_layers not in _NC_CACHE:
        _NC_CACHE[n_layers] = build(n_layers)
    return _NC_CACHE[n_layers]


FUSED = True


def kernel(x, mem, norm_g, w_in, w_out, mem_norm_g, w_mem_kv, qk_g):
    f = lambda a: np.ascontiguousarray(np.asarray(a, dtype=np.float32))
    x, mem, norm_g, w_in, w_out, mem_norm_g, w_mem_kv, qk_g = map(f, (x, mem, norm_g, w_in, w_out, mem_norm_g, w_mem_kv, qk_g))
    consts = make_consts()
    depth = w_in.shape[0]
    groups = [list(range(depth))] if FUSED else [[l] for l in range(depth)]
    cur = x
    for ls in groups:
        nl = len(ls)
        nc = _get_nc(nl)
        sl = slice(ls[0], ls[-1] + 1)
        in_maps = []
        for b in range(N_CORES):
            m = dict(x=cur[b], mem=mem[b], norm_g=norm_g[sl], w_in=w_in[sl], w_out=w_out[sl], mem_norm_g=mem_norm_g[sl],
                     w_mem_kv=w_mem_kv[sl], qk_g=qk_g[sl].reshape(nl, 384))
            m.update(consts)
            in_maps.append(m)
        res = run_bass_kernel_spmd(nc, in_maps, core_ids=list(range(N_CORES)))
        cur = np.stack([np.asarray(r["out"], dtype=np.float32) for r in res.results], axis=0)
    return cur
```

```python
import numpy as np
from contextlib import ExitStack
import concourse.bass as bass
import concourse.mybir as mybir
from concourse.bass_utils import run_bass_kernel_spmd

F32 = mybir.dt.float32
BF16 = mybir.dt.bfloat16
AF = mybir.ActivationFunctionType
ALU = mybir.AluOpType
AX = mybir.AxisListType

T = 2048
D = 1024
DIN = 3584
NMEM = 256
NT = T // 128
NG = T // 512
EPS = 1e-6
BIG = 30000.0
N_CORES = 8

SAME_ENGINE_SYNC = True
SKEW_ON = 3
PRE_A = True
DEFER_FG = True
ATT_SKEW = 7
INTERLEAVE_V = False
PRE_V = True


class Sched:
    ENGS = ("pe", "act", "dve", "pool", "sp")

    def __init__(self, nc, es, n_dma_sp=8, n_dma_pool=20, n_dma_act=2):
        self.nc = nc
        self.ops = {e: [] for e in self.ENGS}
        self.sem = {e: es.enter_context(nc.semaphore("s_" + e)) for e in self.ENGS}
        self.cnt = {e: 0 for e in self.ENGS}
        self.waited = {e: {} for e in self.ENGS}
        self.last_w = {}
        self.readers = {}
        self.semobj = {}
        for e in self.ENGS:
            self.semobj[("c", e)] = self.sem[e]
        self.dma_slots = {}
        self.dma_rr = {}
        for q, n in (("sp", n_dma_sp), ("pool", n_dma_pool), ("act", n_dma_act)):
            sl = []
            for i in range(n):
                key = ("d", q, i)
                self.semobj[key] = es.enter_context(nc.semaphore("d_%s%d" % (q, i)))
                sl.append([key, 0])
            self.dma_slots[q] = sl
            self.dma_rr[q] = 0
        self.out_waits = []

    def _deps(self, eng, reads, writes):
        deps = []
        for r in reads:
            if r in self.last_w:
                deps.append(self.last_w[r])
        for w in writes:
            if w in self.last_w:
                deps.append(self.last_w[w])
            deps.extend(self.readers.get(w, {}).items())
        wd = self.waited[eng]
        own = ("c", eng)
        m = {}
        for (key, val) in deps:
            if key == own and (eng == "pe" or not SAME_ENGINE_SYNC):
                continue
            if wd.get(key, 0) >= val:
                continue
            m[key] = max(m.get(key, 0), val)
        for k, v in m.items():
            wd[k] = v
        return list(m.items())

    def _commit(self, prod, reads, writes):
        for r in reads:
            d = self.readers.setdefault(r, {})
            d[prod[0]] = max(d.get(prod[0], 0), prod[1])
        for w in writes:
            self.last_w[w] = prod
            self.readers[w] = {}

    def op(self, eng, fn, reads=(), writes=(), inc=True):
        waits = self._deps(eng, reads, writes)
        if inc:
            self.cnt[eng] += 1
            prod = (("c", eng), self.cnt[eng])
            self.ops[eng].append((waits, fn, prod[0], 1))
        else:
            prod = (("c", eng), self.cnt[eng] + 1)
            self.ops[eng].append((waits, fn, prod[0], 0))
        self._commit(prod, reads, writes)
        return prod

    def dma(self, q, fn, reads=(), writes=(), is_output=False):
        sl = self.dma_slots[q]
        i = self.dma_rr[q]
        self.dma_rr[q] = (i + 1) % len(sl)
        key, tot = sl[i]
        waits = self._deps(q, reads, writes)
        wd = self.waited[q]
        if tot > 0 and wd.get(key, 0) < tot:
            wd[key] = tot
            waits = [(k, v) for (k, v) in waits if k != key] + [(key, tot)]
        sl[i][1] = tot + 16
        prod = (key, tot + 16)
        self.ops[q].append((waits, fn, key, 16))
        self._commit(prod, reads, writes)
        if is_output:
            self.out_waits.append(prod)
        return prod

    def emit(self):
        nc = self.nc
        m = {}
        for k, v in self.out_waits:
            m[k] = max(m.get(k, 0), v)
        final_waits = list(m.items())
        with nc.Block() as block:
            def run(engname, eng):
                for (waits, fn, key, inc) in self.ops[engname]:
                    for (k, v) in waits:
                        eng.wait_ge(self.semobj[k], v)
                    inst = fn(eng)
                    if inc:
                        inst.then_inc(self.semobj[key], inc)
                if engname == "sp":
                    for (k, v) in final_waits:
                        eng.wait_ge(self.semobj[k], v)

            @block.tensor
            def _(e):
                run("pe", e)

            @block.scalar
            def _(e):
                run("act", e)

            @block.vector
            def _(e):
                run("dve", e)

            @block.gpsimd
            def _(e):
                run("pool", e)

            @block.sync
            def _(e):
                run("sp", e)


class Rot:
    def __init__(self, items):
        self.items = items
        self.i = 0

    def next(self):
        it = self.items[self.i]
        self.i = (self.i + 1) % len(self.items)
        return it


def MM(out, lhsT, rhs, start, stop):
    return lambda e: e.matmul(out, lhsT=lhsT, rhs=rhs, start=start, stop=stop, skip_group_check=True)


def TR(out, in_, ident):
    return lambda e: e.transpose(out=out, in_=in_, identity=ident)


def ACT(out, in_, func, scale=None, accum_out=None, bias=None):
    kw = {}
    if bias is not None:
        kw["bias"] = bias
    if scale is not None:
        kw["scale"] = scale
    if accum_out is not None:
        kw["accum_out"] = accum_out
    return lambda e: e.activation(out=out, in_=in_, func=func, **kw)


def TT(out, in0, in1, op):
    return lambda e: e.tensor_tensor(out=out, in0=in0, in1=in1, op=op)


def TS(out, in0, s1, s2, op0, op1):
    return lambda e: e.tensor_scalar(out=out, in0=in0, scalar1=s1, scalar2=s2, op0=op0, op1=op1)


def STT(out, in0, scalar, in1, op0, op1):
    return lambda e: e.scalar_tensor_tensor(out=out, in0=in0, scalar=scalar, in1=in1, op0=op0, op1=op1)


def RED(out, in_, op):
    return lambda e: e.tensor_reduce(out=out, in_=in_, axis=AX.X, op=op)


def CP(out, in_):
    return lambda e: e.tensor_copy(out=out, in_=in_)


def MS(ap, v):
    return lambda e: e.memset(ap, v)


def RCP(out, in_):
    return lambda e: e.reciprocal(out=out, in_=in_)


def DMA(out, in_, **kw):
    return lambda e: e.dma_start(out=out, in_=in_, **kw)


SEGS = [
    ("a_q", 0, 384, "qk", dict(g=0, dst="Q", p0=0)),
    ("a_k", 384, 384, "qk", dict(g=1, dst="K", p0=0)),
    ("b_k", 1920, 384, "qk", dict(g=3, dst="K", p0=3)),
    ("m_q", 3072, 256, "qn", dict(g=4, dst="Q", p0=6)),
    ("b_q", 1536, 384, "qk", dict(g=2, dst="Q", p0=3)),
    ("a_v", 768, 384, "v", dict(h0=0)),
    ("b_v", 2304, 384, "v", dict(h0=6)),
    ("a_g", 1152, 384, "g", dict(h0=0)),
    ("b_g", 2688, 384, "g", dict(h0=6)),
    ("m_g", 3328, 256, "g", dict(h0=12)),
]


class _Stop(Exception):
    pass


STOP_AT = None


def _chk(tag):
    if STOP_AT is not None and tag == STOP_AT:
        raise _Stop()


def build(n_layers=2):
    nc = bass.Bass("TRN2", target_bir_lowering=False)

    def din(name, shape):
        return nc.dram_tensor(name, shape, F32, kind="ExternalInput").ap()

    x_d = din("x", [T, D])
    mem_d = din("mem", [NMEM, D])
    ng_d = din("norm_g", [n_layers, D])
    win_d = din("w_in", [n_layers, D, DIN])
    wout_d = din("w_out", [n_layers, D, D])
    mng_d = din("mem_norm_g", [n_layers, D])
    wmem_d = din("w_mem_kv", [n_layers, D, 512])
    qkg_d = din("qk_g", [n_layers, 384])
    cid_d = din("c_ident", [128, 128])
    cmd_d = din("c_mdil", [128, 1152])
    ctr_d = din("c_tri", [128, 128])
    ceb_d = din("c_eblk", [128, 16])
    ccos_d = din("c_cos", [T, 64])
    csin_d = din("c_sin", [T, 64])
    out_d = nc.dram_tensor("out", [T, D], F32, kind="ExternalOutput").ap()
    xmid_d = [nc.dram_tensor("xmid%d" % i, [T, D], F32, kind="Internal").ap() for i in range(n_layers - 1)]

    with ExitStack() as es:
        S = Sched(nc, es)

        def SB(name, shape, dt):
            return es.enter_context(nc.sbuf_tensor(name, shape, dt))

        Win = SB("Win", [128, 8, DIN], BF16)
        Wout = SB("Wout", [128, 8, D], BF16)
        yT = SB("yT", [128, 8, 512], BF16)
        hT = SB("hT", [128, 8, 512], BF16)
        KT = SB("KT", [128, 6, T], BF16)
        mKTs = [SB("mKT%d" % l, [128, 4, NMEM], BF16) for l in range(n_layers)]
        mkg = SB("mkg", [128, n_layers, 64], F32)
        Vc = SB("Vc", [128, NT, 12, 65], BF16)
        mVs = [SB("mV%d" % l, [128, 2, 4, 65], BF16) for l in range(n_layers)]
        ident = SB("ident", [128, 128], BF16)
        mdil = SB("mdil", [128, 1152], BF16)
        tri = SB("tri", [128, 128], BF16)
        eblk = SB("eblk", [128, 2, 8], BF16)
        cosb = SB("cosb", [128, NT, 64], F32)
        sinb = SB("sinb", [128, NT, 64], F32)
        gbuf = SB("gbuf", [128, D], F32)
        qkg = SB("qkg", [128, 6, 64], F32)
        mhalf = SB("mhalf", [128, 8], F32)
        xa = [SB("xa%d" % i, [128, D], F32) for i in range(2)]
        hbs = [SB("hb%d" % i, [128, D], BF16) for i in range(2)]
        QT = SB("QT", [128, 8, 512], BF16)
        Gt = SB("Gt", [128, 4, D], BF16)
        ptall = SB("ptall", [128, 8, 512], BF16)
        pts = [ptall[:, i, :] for i in range(8)]
        junk = ptall[:, 0:2, :].rearrange("p a c -> p (a c)")
        scrA_all = SB("scrA_all", [128, 2, 384], F32)
        scrA = [scrA_all[:, i, :] for i in range(2)]
        junk2 = scrA_all[:].rearrange("p a c -> p (a c)").bitcast(BF16)[:, 0:D]
        scrB = [SB("scrB%d" % i, [128, 384], F32) for i in range(2)]
        epsb = SB("epsb", [128, 1], F32)
        qktm = [SB("qktm%d" % i, [128, 384], BF16) for i in range(2)]
        biasT = SB("biasT", [128, 3, 512], BF16)
        kmT = SB("kmT", [128, 3, 8], BF16)
        ksum = SB("ksum", [128, 4], F32)
        gsb = SB("gsb", [128, 96], F32)
        rank = SB("rank", [128, 96], F32)
        bqs = [SB("bq%d" % i, [128, 96], BF16) for i in range(2)]
        st = SB("st", [128, 64], F32)
        rc = SB("rc", [128, 4], F32)
        print('SBUF free bytes/partition after alloc:', nc.sbuf_bytes_remaining)
        ps = [es.enter_context(nc.psum_tensor("ps%d" % i, [128, 512], F32)) for i in range(8)]

        pjr = Rot([("ps0", ps[0]), ("ps1", ps[1])])
        tpr = Rot([("ps2", ps[2]), ("ps3", ps[3])])
        str_ = Rot([("ps0", ps[0]), ("ps1", ps[1]), ("ps4", ps[4]), ("ps5", ps[5])])
        oar = Rot([("ps6", ps[6]), ("ps7", ps[7])])
        xar = Rot([(["xa0", "xC0", "xT0", "xT0a", "xT0b", "xH0a", "xH0b"], xa[0]), (["xa1", "xC1", "xT1", "xT1a", "xT1b", "xH1a", "xH1b"], xa[1])])
        ptr = Rot([("pt%d" % i, pts[i]) for i in range(8)])
        qkr = Rot([("qktm%d" % i, qktm[i]) for i in range(2)])
        scr = Rot([0, 1])
        bqr = Rot([("bq%d" % i, bqs[i]) for i in range(2)])
        scr3 = Rot([0, 1, 2])
        nsr = Rot([0, 1])
        hbr = Rot([("hb%d" % i, hbs[i]) for i in range(2)])

        S.dma("pool", DMA(ident[:], cid_d), writes=["ident"])
        S.dma("pool", DMA(mdil[:], cmd_d), writes=["mdil"])
        S.dma("pool", DMA(tri[:], ctr_d), writes=["tri"])
        S.dma("pool", DMA(eblk[:], ceb_d.rearrange("n (t a) -> n t a", t=2)), writes=["eblk"])
        S.dma("sp", DMA(cosb[:], ccos_d.rearrange("(t p) c -> p t c", p=128)), writes=["cos"])
        S.dma("sp", DMA(sinb[:], csin_d.rearrange("(t p) c -> p t c", p=128)), writes=["sin"])
        S.op("pool", MS(mhalf[:], -0.5), writes=["mhalf"])
        S.op("pool", MS(epsb[:], EPS), writes=["epsb"])
        S.op("pool", MS(QT[:], 0.0), writes=[("QT", h_, i_) for h_ in range(8) for i_ in range(4)])
        for l in range(n_layers):
            S.op("pool", MS(mKTs[l][:], 0.0), writes=[("mKT", l, 0), ("mKT", l, 1)])
            S.op("pool", MS(mVs[l][:, :, :, 64:65], 1.0), writes=[("mVones", l)])
            S.dma("sp", DMA(mkg[:, l, :], qkg_d[l:l + 1, 320:384].partition_broadcast(128)), writes=[("mkg", l)])
        S.op("pool", MS(kmT[:], 0.0), writes=[("kmT", n) for n in range(8)])
        S.op("pool", MS(biasT[:], 0.0), writes=[("biasT", i) for i in range(4)])
        S.op("pool", MS(Vc[:, :, :, 64:65], 1.0), writes=["Vones"])

        def h3(ap, nh):
            return ap.rearrange("p (h d) -> p h d", d=64) if len(ap.shape) == 2 else ap

        def norm_stages(src_ap, src_res, col0, ht_res, alt_junk=False):
            hold = {}

            def s0():
                xres, xt = xar.next()
                hres, hbt = hbr.next()
                sp = nsr.next()
                hold.update(x=(xres, xt), h=(hres, hbt), sp=sp)
                o = 56 + sp * 4
                S.dma("sp", DMA(xt[:], src_ap), reads=src_res, writes=xres)
                if alt_junk:
                    S.op("act", ACT(junk2, xt[:], AF.Square, accum_out=st[:, o:o + 1]), reads=xres, writes=["scrA0", "scrA1", "nss%d" % sp])
                else:
                    S.op("act", ACT(junk, xt[:], AF.Square, accum_out=st[:, o:o + 1]), reads=xres, writes=["pt0", "pt1", "nss%d" % sp])
                S.op("act", ACT(st[:, o + 1:o + 2], st[:, o:o + 1], AF.Ln, scale=1.0 / D, bias=epsb[:, 0:1]), reads=["nss%d" % sp, "epsb"], writes=["nlv%d" % sp])
                S.op("act", ACT(st[:, o + 2:o + 3], st[:, o + 1:o + 2], AF.Exp, scale=-0.5), reads=["nlv%d" % sp], writes=["nrs%d" % sp])

            def s1():
                xres, xt = hold["x"]
                hres, hbt = hold["h"]
                o = 56 + hold["sp"] * 4
                S.op("dve", STT(hbt[:], xt[:], st[:, o + 2:o + 3], gbuf[:], ALU.mult, ALU.mult), reads=xres + ["nrs%d" % hold["sp"], "gbuf"], writes=[hres])

            def s2():
                hres, hbt = hold["h"]
                tres, tp = tpr.next()
                tpb = tp[:].bitcast(BF16)
                for k in range(8):
                    S.op("pe", TR(tpb[:, k * 128:(k + 1) * 128], hbt[:, k * 128:(k + 1) * 128], ident[:]), reads=[hres, "ident"], writes=[tres], inc=(k == 7))
                S.op("act", ACT(hT[:, :, col0:col0 + 128], tpb.rearrange("p (k c) -> p k c", c=128), AF.Copy), reads=[tres], writes=ht_res)

            return (s0, s1, s2, None)

        def norm_parts(src_ap, src_res, col0, ht_res):
            hold = {}

            def dma():
                xres, xt = xar.next()
                hold["x"] = (xres, xt)
                S.dma("sp", DMA(xt[:], src_ap), reads=src_res, writes=xres)

            def act():
                xres, xt = hold["x"]
                hres, hbt = hbr.next()
                sp = nsr.next()
                hold.update(h=(hres, hbt), sp=sp)
                o = 56 + sp * 4
                S.op("dve", (lambda o_, x_, a_: (lambda e: e.scalar_tensor_tensor(out=o_, in0=x_, scalar=1.0, in1=x_, op0=ALU.mult, op1=ALU.mult, accum_out=a_)))(junk2, xt[:], st[:, o:o + 1]),
                     reads=xres, writes=["scrA0", "scrA1", "nss%d" % sp])
                S.op("act", ACT(st[:, o + 1:o + 2], st[:, o:o + 1], AF.Ln, scale=1.0 / D, bias=epsb[:, 0:1]), reads=["nss%d" % sp, "epsb"], writes=["nlv%d" % sp])
                S.op("act", ACT(st[:, o + 2:o + 3], st[:, o + 1:o + 2], AF.Exp, scale=-0.5), reads=["nlv%d" % sp], writes=["nrs%d" % sp])

            def dve():
                xres, xt = hold["x"]
                hres, hbt = hold["h"]
                o = 56 + hold["sp"] * 4
                S.op("dve", STT(hbt[:], xt[:], st[:, o + 2:o + 3], gbuf[:], ALU.mult, ALU.mult), reads=xres + ["nrs%d" % hold["sp"], "gbuf"], writes=[hres])

            def tr():
                hres, hbt = hold["h"]
                tres, tp = tpr.next()
                tpb = tp[:].bitcast(BF16)
                for k in range(8):
                    S.op("pe", TR(tpb[:, k * 128:(k + 1) * 128], hbt[:, k * 128:(k + 1) * 128], ident[:]), reads=[hres, "ident"], writes=[tres], inc=(k == 7))
                S.op("dve", CP(hT[:, :, col0:col0 + 128], tpb.rearrange("p (k c) -> p k c", c=128)), reads=[tres], writes=ht_res)

            return dict(dma=dma, act=act, dve=dve, tr=tr)

        def norm_tile(src_ap, src_res, col0, ht_res):
            a, b, c, _ = norm_stages(src_ap, src_res, col0, ht_res)
            a()
            b()
            c()

        def evac_qk_slots(ps_ap, ps_res, nh, gidx, rope_tile, dest_ap, dest_res):
            w = nh * 64
            sp = scr.next()
            A, B = scrA[sp], scrB[sp]
            rA, rB = "scrA%d" % sp, "scrB%d" % sp
            o = 8 + sp * 24
            ss, lv, rs = st[:, o:o + nh], st[:, o + 8:o + 8 + nh], st[:, o + 16:o + 16 + nh]
            rss, rlv, rrs = "ss%d" % sp, "lv%d" % sp, "rs%d" % sp
            ps3 = ps_ap.rearrange("p (h d) -> p h d", d=64)
            A3 = A[:, 0:w].rearrange("p (h d) -> p h d", d=64)
            B3 = B[:, 0:w].rearrange("p (h d) -> p h d", d=64)
            hold = {}
            Cb = xa[sp][:, 0:w].rearrange("p (h d) -> p h d", d=64)
            TA = xa[sp][:, 512:576]
            TB = xa[sp][:, 576:640]
            H3 = xa[sp][:, 640:640 + w].rearrange("p (h d) -> p h d", d=64)
            rC, rT, rH = "xC%d" % sp, "xT%d" % sp, "xH%d" % sp
            if isinstance(gidx, tuple):
                gv, gres = mkg[:, gidx[1], :], ("mkg", gidx[1])
            else:
                gv, gres = qkg[:, gidx, :], "qkg"
            rope = rope_tile is not None
            sl = {}

            def act_sq():
                S.op("act", ACT(A[:, 0:w], ps_ap, AF.Square), reads=[ps_res], writes=[rA])
            sl["act_sq"] = act_sq

            def dve_red():
                S.op("dve", RED(ss, A3, ALU.add), reads=[rA], writes=[rss])
            sl["dve_red"] = dve_red

            def act_lnexp():
                S.op("act", ACT(lv, ss, AF.Ln, scale=1.0 / 64, bias=epsb[:, 0:1]), reads=[rss, "epsb"], writes=[rlv])
                S.op("act", ACT(rs, lv, AF.Exp, scale=-0.5), reads=[rlv], writes=[rrs])
            sl["act_lnexp"] = act_lnexp

            def dve_tab():
                S.op("pool", TT(TA, cosb[:, rope_tile, :], gv, ALU.mult), reads=["cos", gres], writes=[rT])
                S.op("pool", TT(TB[:, 0:32], sinb[:, rope_tile, 0:32], gv[:, 32:64], ALU.mult), reads=["sin", gres], writes=[rT + "a"])
                S.op("pool", TT(TB[:, 32:64], sinb[:, rope_tile, 32:64], gv[:, 0:32], ALU.mult), reads=["sin", gres], writes=[rT + "b"])
            if rope:
                sl["dve_tab"] = dve_tab

            def dve_BC():
                S.op("dve", TT(B3, ps3, rs.unsqueeze(2).broadcast_to([128, nh, 64]), ALU.mult), reads=[ps_res, rrs], writes=[rB])
                qres, qb = qkr.next()
                hold["q"] = (qres, qb)
                qb3 = qb[:, 0:w].rearrange("p (h d) -> p h d", d=64)
                if not rope:
                    pass
                else:
                    S.op("dve", TT(Cb, B3, TA.unsqueeze(1).broadcast_to([128, nh, 64]), ALU.mult), reads=[rB, rT], writes=[rC])
            sl["dve_BC"] = dve_BC

            def pool_halves():
                H4 = xa[sp][:, 640:640 + w].rearrange("p (h j d) -> p h j d", j=2, d=32)
                Br = B[:, 0:w].rearrange("p (h j d) -> p h j d", j=2, d=32)[:, :, ::-1, :]
                TB4 = TB.rearrange("p (j d) -> p j d", d=32).unsqueeze(1).broadcast_to([128, nh, 2, 32])
                S.op("pool", TT(H4, Br, TB4, ALU.mult), reads=[rB, rT + "a", rT + "b"], writes=[rH + "a", rH + "b"])
            if rope:
                sl["pool_halves"] = pool_halves

            def dve_add():
                qres, qb = hold["q"]
                qb3 = qb[:, 0:w].rearrange("p (h d) -> p h d", d=64)
                if rope:
                    S.op("dve", TT(qb3, Cb, H3, ALU.add), reads=[rC, rH + "a", rH + "b"], writes=[qres])
                else:
                    S.op("dve", TT(qb3, B3, gv.unsqueeze(1).broadcast_to([128, nh, 64]), ALU.mult), reads=[rB, gres], writes=[qres])
            sl["dve_add"] = dve_add

            def pe_tr():
                qres, qb = hold["q"]
                tres, tp = tpr.next()
                tpb = tp[:].bitcast(BF16)
                hold["tp"] = (tres, tpb)
                for pi in range(nh // 2):
                    S.op("pe", TR(tpb[:, pi * 128:(pi + 1) * 128], qb[:, pi * 128:(pi + 1) * 128], ident[:]), reads=[qres, "ident"], writes=[tres], inc=(pi == nh // 2 - 1))
            sl["pe_tr"] = pe_tr

            def act_copy():
                tres, tpb = hold["tp"]
                if callable(dest_ap):
                    dest_ap(tpb, tres)
                else:
                    S.op("act", ACT(dest_ap, tpb[:, 0:(nh // 2) * 128].rearrange("p (a c) -> p a c", c=128), AF.Copy), reads=[tres], writes=dest_res)
            sl["act_copy"] = act_copy
            return sl

        SLOT_SCHED = [("dve_add", 6), ("act_sq", 1), ("pe_tr", 7), ("dve_red", 2), ("act_lnexp", 3),
                      ("dve_BC", 4), ("pool_halves", 4), ("dve_tab", 3), ("act_copy", 8), ("pe_mm", 0)]
        SLOT_DEPTH = 8
        SLOT_LOGICAL = ["pe_mm", "act_sq", "dve_red", "act_lnexp", "dve_tab", "dve_BC", "pool_halves", "dve_add", "pe_tr", "act_copy"]

        def run_slots(units, extras=None):
            n = len(units)
            extras = extras or {}
            if not (SKEW_ON & 2):
                for u in units:
                    for nm in SLOT_LOGICAL:
                        if nm in u:
                            u[nm]()
                for t in sorted(extras):
                    for f in extras[t]:
                        f()
                return
            for t in range(max(n + SLOT_DEPTH, max(extras) + 1 if extras else 0)):
                for (nm, k) in SLOT_SCHED:
                    u = t - k
                    if 0 <= u < n and nm in units[u]:
                        units[u][nm]()
                for f in extras.get(t, []):
                    f()

        def evac_qk(*a):
            sl = evac_qk_slots(*a)
            for nm in SLOT_LOGICAL:
                if nm in sl:
                    sl[nm]()

        def run_skewed(units, skew=1):
            n = len(units)
            ns = max(len(u) for u in units)
            if not (SKEW_ON & skew):
                for u in units:
                    for f in u:
                        if f is not None:
                            f()
                return
            for t in range(n + ns - 1):
                for k in range(ns - 1, -1, -1):
                    u = t - k
                    if 0 <= u < n and k < len(units[u]) and units[u][k] is not None:
                        units[u][k]()

        pj4 = Rot([("ps0", ps[0]), ("ps1", ps[1]), ("ps4", ps[4]), ("ps5", ps[5]), ("ps6", ps[6])])

        def make_unit(G, il, c0, w, kind, ex, bankrot=None):
            tile = 4 * G + il
            hold = {}
            u = {}

            def pe_mm():
                pres, pj = (bankrot or pj4).next()
                hold["pj"] = (pres, pj)
                for k in range(8):
                    S.op("pe", MM(pj[:, 0:w], hT[:, k, il * 128:(il + 1) * 128], Win[:, k, c0:c0 + w], k == 0, k == 7),
                         reads=[("hT", il), ("Win", c0)], writes=[pres], inc=(k == 7))
            u["pe_mm"] = pe_mm

            if kind in ("qk", "qn"):
                nh = w // 64
                npair = nh // 2
                p0 = ex["p0"]
                if ex["dst"] == "Q" and kind == "qk":
                    base = QT if p0 == 0 else yT
                    zero_too = (p0 != 0)
                    dres = [("QT", i, il) for i in range(6)] if p0 == 0 else [("yT", il)]
                    bv = base[:, 0:6, il * 128:(il + 1) * 128].rearrange("p (a two) c -> p a two c", two=2)

                    def dest(tpb, tres, bv=bv, dres=dres, zero_too=zero_too):
                        src = tpb[:, 0:384].rearrange("p (a c) -> p a c", c=128)
                        for par in range(2):
                            rows = slice(par * 64, par * 64 + 64)
                            S.op("act", ACT(bv[rows, :, par, :], src[rows], AF.Copy), reads=[tres], writes=dres)
                            if zero_too:
                                orow = slice((1 - par) * 64, (1 - par) * 64 + 64)
                                S.op("act", ACT(bv[orow, :, par, :], src[orow], AF.Copy, scale=0.0), reads=[tres], writes=dres)
                elif ex["dst"] == "Q":
                    dest = QT[:, p0:p0 + npair, il * 128:(il + 1) * 128]
                    dres = [("QT", p0 + i, il) for i in range(npair)]
                else:
                    dest = KT[:, p0:p0 + npair, tile * 128:(tile + 1) * 128]
                    dres = [("KT", p0 + i, tile) for i in range(npair)]

                def lazy(nm):
                    def f():
                        if "sl" not in hold:
                            pres, pj = hold["pj"]
                            hold["sl"] = evac_qk_slots(pj[:, 0:w], pres, nh, ex["g"], tile if kind == "qk" else None, dest, dres)
                        if nm in hold["sl"]:
                            hold["sl"][nm]()
                    return f
                for nm in SLOT_LOGICAL[1:]:
                    if kind == "qn" and nm in ("dve_tab", "pool_halves"):
                        continue
                    u[nm] = lazy(nm)
            elif kind == "v":
                h0 = ex["h0"]

                def vcopy():
                    pres, pj = hold["pj"]
                    if il % 2 == 0:
                        S.op("act", ACT(Vc[:, tile, h0:h0 + 6, 0:64], pj[:, 0:w].rearrange("p (h d) -> p h d", d=64), AF.Copy),
                             reads=[pres, "Vones"], writes=[("V", tile, h0 // 6)])
                    else:
                        S.op("dve", CP(Vc[:, tile, h0:h0 + 6, 0:64], pj[:, 0:w].rearrange("p (h d) -> p h d", d=64)),
                             reads=[pres, "Vones"], writes=[("V", tile, h0 // 6)])
                u["dve_red"] = vcopy
            else:
                h0 = ex["h0"]

                def gsilu():
                    pres, pj = hold["pj"]
                    S.op("act", ACT(Gt[:, il, h0 * 64:h0 * 64 + w], pj[:, 0:w], AF.Silu), reads=[pres],
                         writes=[("Gt", il, h0 + i) for i in range(w // 64)])
                u["act_sq"] = gsilu
            return u

        for L in range(n_layers):
            S.dma("pool", DMA(yT[:], wmem_d[L].rearrange("(k p) c -> p k c", p=128)), writes=[("yT", i) for i in range(4)])
            if L == n_layers - 1:
                for (sname, c0, w, kind, ex) in SEGS:
                    S.dma("pool", DMA(Win[:, :, c0:c0 + w], win_d[0][:, c0:c0 + w].rearrange("(k p) c -> p k c", p=128)), writes=[("Win", c0)])
            S.dma("sp", DMA(gbuf[:], mng_d[L:L + 1, :].partition_broadcast(128)), writes=["gbuf"])
            def mem_unit(mt, L=L):
                n0, n1, n2, _ = norm_stages(mem_d[mt * 128:(mt + 1) * 128, :], [], mt * 128, [("hT", mt)])
                hold = {}

                def mm():
                    pres, pj = pjr.next()
                    hold["pj"] = (pres, pj)
                    for k in range(8):
                        S.op("pe", MM(pj[:, 0:512], hT[:, k, mt * 128:(mt + 1) * 128], yT[:, k, :], k == 0, k == 7),
                             reads=[("hT", mt)] + [("yT", i) for i in range(4)], writes=[pres], inc=(k == 7))

                mkv = mKTs[L][:, :, mt * 128:(mt + 1) * 128].rearrange("p (a two) c -> p a two c", two=2)

                def mdest(tpb, tres):
                    src = tpb[:, 0:256].rearrange("p (a c) -> p a c", c=128)
                    for par in range(2):
                        rows = slice(par * 64, par * 64 + 64)
                        S.op("act", ACT(mkv[rows, :, par, :], src[rows], AF.Copy), reads=[tres], writes=[("mKT", L, mt)])

                def ev(names, extra=None):
                    def f():
                        if "sl" not in hold:
                            pres, pj = hold["pj"]
                            hold["sl"] = evac_qk_slots(pj[:, 0:256], pres, 4, ("mkg", L), None, mdest, [("mKT", L, mt)])
                        for nm in names:
                            if nm in hold["sl"]:
                                hold["sl"][nm]()
                        if extra:
                            extra()
                    return f

                def vcopy():
                    pres, pj = hold["pj"]
                    S.op("dve", CP(mVs[L][:, mt, :, 0:64], pj[:, 256:512].rearrange("p (h d) -> p h d", d=64)), reads=[pres, ("mVones", L)], writes=[("mV", L, mt)])

                return (n0, n1, n2, mm, ev(["act_sq", "dve_red"], vcopy), ev(["act_lnexp"]), ev(["dve_BC"]), ev(["dve_add"]), ev(["pe_tr"]), ev(["act_copy"]))

            run_skewed([mem_unit(0), mem_unit(1)])

        def make_fg_phases(Gp, src_p, src_id_p, dst_p, dst_id_p, last_p):
            phases = []
            tpb7 = ps[7][:].bitcast(BF16)

            def xload(il, half):
                t_ = 4 * Gp + il
                S.dma("sp", DMA(hbs[half][:].bitcast(F32), src_p[t_ * 128:(t_ + 1) * 128, half * 512:(half + 1) * 512]),
                      reads=[("xd", src_id_p, t_)], writes=["hb%d" % half])

            for il in range(4):
                tile = 4 * Gp + il

                def p_tr(il=il):
                    if il == 0:
                        xload(0, 0)
                        xload(0, 1)
                    for k in range(8):
                        S.op("pe", TR(tpb7[:, k * 128:(k + 1) * 128], Gt[:, il, k * 128:(k + 1) * 128], ident[:]),
                             reads=[("Gt", il, 2 * k), ("Gt", il, 2 * k + 1), "ident"], writes=["ps7"], inc=(k == 7))

                def p_cp(il=il):
                    S.op("act", ACT(yT[:, :, il * 128:(il + 1) * 128], tpb7.rearrange("p (k c) -> p k c", c=128), AF.Copy), reads=["ps7"], writes=[("yT", il)])

                def p_mm(half, il=il):
                    for k in range(8):
                        S.op("pe", MM(ps[7][:, 0:512], yT[:, k, il * 128:(il + 1) * 128], Wout[:, k, half * 512:(half + 1) * 512], k == 0, k == 7),
                             reads=[("yT", il), ("Wout", k)], writes=["ps7"], inc=(k == 7))

                def p_add(half, il=il, tile=tile):
                    xt = hbs[half][:].bitcast(F32)
                    S.op("dve", TT(xt, ps[7][:, 0:512], xt, ALU.add), reads=["ps7", "hb%d" % half], writes=["hb%d" % half])
                    S.dma("sp", DMA(dst_p[tile * 128:(tile + 1) * 128, half * 512:(half + 1) * 512], xt), reads=["hb%d" % half],
                          writes=[("xd", dst_id_p, tile)], is_output=last_p)
                    if il + 1 < 4:
                        xload(il + 1, half)

                phases += [p_tr, p_cp, (lambda il=il: p_mm(0, il)), (lambda il=il, tile=tile: p_add(0, il, tile)),
                           (lambda il=il: p_mm(1, il)), (lambda il=il, tile=tile: p_add(1, il, tile))]
            return phases

        pending_fg = []
        pre_a = {"done": False}
        pre_v = {"done": False}
        ps1r = Rot([("ps1", ps[1])])
        try:
          for L in range(n_layers):
              src_d = x_d if L == 0 else xmid_d[L - 1]
              src_id = "x" if L == 0 else "mid%d" % (L - 1)
              last = (L == n_layers - 1)
              dst_d = out_d if last else xmid_d[L]
              dst_id = "out" if last else "mid%d" % L

              if L > 0:
                  for (sname, c0, w, kind, ex) in SEGS:
                      S.dma("pool", DMA(Win[:, :, c0:c0 + w], win_d[L][:, c0:c0 + w].rearrange("(k p) c -> p k c", p=128)), writes=[("Win", c0)])
              if not pre_a["done"]:
                  S.dma("sp", DMA(gbuf[:], ng_d[L:L + 1, :].partition_broadcast(128)), writes=["gbuf"])
              S.dma("sp", DMA(qkg[:].rearrange("p a d -> p (a d)"), qkg_d[L:L + 1, :].partition_broadcast(128)), writes=["qkg"])
              mKT, mV = mKTs[L], mVs[L]

              _chk('mem')
              for G in range(NG):
                  if not pre_a["done"]:
                      run_skewed([norm_stages(src_d[(4 * G + il) * 128:(4 * G + il + 1) * 128, :], [("xd", src_id, 4 * G + il)], il * 128, [("hT", il)])
                                  for il in range(4)])
                  pre_a["done"] = False
                  _chk('a%d' % G)
                  uq, uv, ug = [], [], []
                  for (sname, c0, w, kind, ex) in SEGS:
                      if kind == "v" and pre_v["done"]:
                          continue
                      for il in range(4):
                          u = make_unit(G, il, c0, w, kind, ex)
                          (uq if kind in ("qk", "qn") else uv if kind == "v" else ug).append(u)
                  pre_v["done"] = False
                  units = []
                  for i, u in enumerate(uq):
                      units.append(u)
                      if INTERLEAVE_V and i % 2 == 1 and uv:
                          units.append(uv.pop(0))
                  units += uv + ug

                  def kmean_ops():
                      for n in (2 * G, 2 * G + 1):
                          S.op("dve", RED(ksum[:, 0:3], KT[:, 0:3, n * 256:(n + 1) * 256], ALU.add),
                               reads=[("KT", p, t) for p in range(3) for t in (2 * n, 2 * n + 1)], writes=["ksum"])
                          S.op("dve",
                               (lambda o, i: (lambda e: e.tensor_scalar_mul(out=o, in0=i, scalar1=1.0 / 256)))(kmT[:, :, n], ksum[:, 0:3]),
                               reads=["ksum"], writes=[("kmT", n)])
                  extras = {25: [kmean_ops]}
                  for i_, f_ in enumerate(pending_fg):
                      extras.setdefault(1 + i_, []).append(f_)
                  del pending_fg[:]
                  gated_group = (2 * G + 1) > 3
                  if gated_group:
                      def gate_unit(b):
                          c = (4 * G + 2 * b) // 2
                          hold = {}
                          g4 = gsb[:].rearrange("p (t s n) -> p t s n", t=2, n=8)
                          g3 = gsb[:].rearrange("p (a n) -> p a n", n=8)

                          def s0():
                              for ti in range(2):
                                  il = 2 * b + ti
                                  for h in range(6):
                                      col = ti * 48 + h * 8
                                      S.op("pe", MM(ps[7][:, col:col + 8], QT[:, h, il * 128:(il + 1) * 128], kmT[:, h // 2, :], True, True),
                                           reads=[("QT", h, il)] + [("kmT", n) for n in range(8)], writes=["ps7"])

                          def s1():
                              hold["bq"] = bqr.next()
                              bres_, bq = hold["bq"]
                              S.op("dve", CP(gsb[:], ps[7][:, 0:96]), reads=["ps7"], writes=["gsb"])
                              S.op("dve", MS(g3[:, :, c:8], -1e30), reads=[], writes=["gsb"])
                              for ti in range(2):
                                  cmf = hbs[ti][:].bitcast(F32)[:, 0:384]
                                  gt = g3[:, ti * 6:(ti + 1) * 6, :]
                                  S.op("dve", TT(cmf.rearrange("p (a n m) -> p a n m", n=8, m=8), gt.unsqueeze(2).broadcast_to([128, 6, 8, 8]),
                                                 gt.unsqueeze(3).broadcast_to([128, 6, 8, 8]), ALU.is_gt), reads=["gsb"], writes=["hb%d" % ti])
                                  S.op("dve", RED(rank[:, ti * 48:(ti + 1) * 48], cmf.rearrange("p (a m) -> p a m", m=8), ALU.add),
                                       reads=["hb%d" % ti], writes=["rank"])
                              S.op("dve", TS(bq[:], rank[:], 3.0, -BIG, ALU.is_ge, ALU.mult), reads=["rank"], writes=[bres_])
                              b3 = bq[:].rearrange("p (a n) -> p a n", n=8)
                              S.op("dve", MS(b3[:, :, c:c + 1], 0.0), reads=[], writes=[bres_])

                          def s2():
                              bres_, bq = hold["bq"]
                              b3 = bq[:].rearrange("p (a n) -> p a n", n=8)
                              tpb2 = ps[7][:].bitcast(BF16)
                              for ti in range(2):
                                  for h in range(6):
                                      par, pr = h % 2, h // 2
                                      S.op("pe", TR(tpb2[par * 64:par * 64 + 8, (ti * 3 + pr) * 128:(ti * 3 + pr + 1) * 128], b3[:, ti * 6 + h, :], ident[:]),
                                           reads=[bres_, "ident"], writes=["ps7"])

                          def s3():
                              tpb2 = ps[7][:].bitcast(BF16)
                              for ti in range(2):
                                  il = 2 * b + ti
                                  for par in range(2):
                                      S.op("dve", CP(biasT[par * 64:par * 64 + 8, :, il * 128:(il + 1) * 128],
                                                     tpb2[par * 64:par * 64 + 8, ti * 384:(ti + 1) * 384].rearrange("p (h c) -> p h c", c=128)),
                                           reads=["ps7"], writes=[("biasT", il)])
                          return (s0, s1, s2, s3)

                      g0, g1 = gate_unit(0), gate_unit(1)
                      for i_, f_ in enumerate(list(g0) + list(g1)):
                          extras.setdefault(27 + 2 * i_, []).append(f_)
                  run_slots(units, extras)
                  if G == 0:
                      for k in range(8):
                          S.dma("pool", DMA(Wout[:, k, :], wout_d[L, k * 128:(k + 1) * 128, :]), writes=[("Wout", k)])
                  _chk('d%d' % G)
                  steps = []
                  for h in range(16):
                      if h < 12:
                          kind = "A" if h < 6 else "B"
                          pair = h // 2
                          nkt = 4 * G + 4
                      else:
                          kind = "M"
                          pair = 6 + (h - 12) // 2
                          nkt = 2
                      for j in range(nkt):
                          steps.append(dict(h=h, kind=kind, pair=pair, nkt=nkt, j=j))
                  SKEW = ATT_SKEW
                  cur_o = {}

                  def emit_S(sd):
                      h, kind, pair, j = sd["h"], sd["kind"], sd["pair"], sd["j"]
                      r0 = (h % 2) * 64
                      qs = 512 * G if kind == "M" else max(512 * G, 128 * j)
                      N = 512 * (G + 1) - qs
                      ql = qs - 512 * G
                      sres, stp = str_.next()
                      il_lo = ql // 128
                      if kind == "M":
                          lhsT = mKT[:, h - 12, j * 128:(j + 1) * 128]
                          kreads = [("mKT", L, j)]
                          rhsq = QT[:, pair, ql:ql + N]
                          qreads = [("QT", pair, i) for i in range(il_lo, 4)]
                      elif kind == "A":
                          lhsT = KT[:, pair, j * 128:(j + 1) * 128]
                          kreads = [("KT", pair, j)]
                          rhsq = QT[:, h, ql:ql + N]
                          qreads = [("QT", h, i) for i in range(il_lo, 4)]
                      else:
                          lhsT = KT[:, pair, j * 128:(j + 1) * 128]
                          kreads = [("KT", pair, j)]
                          rhsq = yT[:, h - 6, ql:ql + N]
                          qreads = [("yT", i) for i in range(il_lo, 4)]
                      n = j // 2
                      gated = (kind == "A") and gated_group and (n <= 2 * G)
                      S.op("pe", MM(stp[:, 0:N], lhsT, rhsq, True, not gated), reads=kreads + qreads, writes=[sres], inc=(not gated))
                      if gated:
                          S.op("pe", MM(stp[:, 0:N], eblk[:, h % 2, n:n + 1].broadcast_to([128, 128]), biasT[:, pair, ql:ql + N], False, True),
                               reads=["eblk"] + [("biasT", i) for i in range(il_lo, 4)], writes=[sres])
                      pres_, pt = ptr.next()
                      S.op("act", ACT(pt[:, 0:N], stp[:, 0:N], AF.Exp, scale=0.125), reads=[sres], writes=[pres_])
                      if kind == "A" and j >= 4 * G:
                          S.op("dve", TT(pt[:, 0:128], pt[:, 0:128], tri[:], ALU.mult), reads=[pres_, "tri"], writes=[pres_])
                      if kind == "B":
                          x0 = min(qs - 128 * j, 640)
                          S.op("dve", TT(pt[:, 0:N], pt[:, 0:N], mdil[:, x0:x0 + N], ALU.mult), reads=[pres_, "mdil"], writes=[pres_])
                      sd["pt"] = (pres_, pt)
                      sd["qs"] = qs

                  def emit_PV(sd):
                      h, kind, j, nkt, qs = sd["h"], sd["kind"], sd["j"], sd["nkt"], sd["qs"]
                      pres_, pt = sd["pt"]
                      if j == 0:
                          cur_o["o"] = oar.next()
                      ores, oa = cur_o["o"]
                      for i in range(qs // 128, 4 * G + 4):
                          il = i - 4 * G
                          c0 = i * 128 - qs
                          if kind == "M":
                              rhs = mV[:, j, h - 12, :]
                              vreads = [("mV", L, j), ("mVones", L)]
                          else:
                              rhs = Vc[:, j, h, :]
                              vreads = [("V", j, h // 6), "Vones"]
                          S.op("pe", MM(oa[:, il * 128:il * 128 + 65], pt[:, c0:c0 + 128], rhs, (j == 0 and i == qs // 128), j == nkt - 1),
                               reads=[pres_] + vreads, writes=[ores], inc=(i == 4 * G + 3))
                      if j == nkt - 1:
                          oa3 = oa[:].rearrange("p (a c) -> p a c", c=128)
                          S.op("dve", RCP(rc[:, 0:4], oa3[:, :, 64]), reads=[ores], writes=["rc"])
                          for il in range(4):
                              gap = Gt[:, il, h * 64:(h + 1) * 64]
                              S.op("dve", STT(gap, oa[:, il * 128:il * 128 + 64], rc[:, il:il + 1], gap, ALU.mult, ALU.mult),
                                   reads=[ores, "rc", ("Gt", il, h)], writes=[("Gt", il, h)])

                  vq = []
                  nsched = {}
                  for idx in range(len(steps) + SKEW):
                      if PRE_A and idx == 0 and not (G == NG - 1 and L == n_layers - 1):
                          if G < NG - 1:
                              nsrc, nsid, nG = src_d, src_id, G + 1
                          else:
                              nsrc, nsid, nG = dst_d, dst_id, 0
                              S.dma("sp", DMA(gbuf[:], ng_d[L + 1:L + 2, :].partition_broadcast(128)), writes=["gbuf"])
                          nparts = [norm_parts(nsrc[(4 * nG + il) * 128:(4 * nG + il + 1) * 128, :], [("xd", nsid, 4 * nG + il)], il * 128, [("hT", il)])
                                    for il in range(4)]
                          nsched = {}
                          for il in range(4):
                              base = (il + 1) * (6 * (4 * G + 4)) // 5
                              for off, nm in ((-5, "dma"), (0, "act"), (1, "dve"), (3, "tr")):
                                  nsched.setdefault(max(0, base + off), []).append(nparts[il][nm])
                          pre_a["done"] = True
                          if PRE_V and G < NG - 1 and False:
                              vq = [make_unit(G + 1, il, c0, w, kind, ex, bankrot=ps1r) for (sname, c0, w, kind, ex) in SEGS if kind == "v" for il in range(4)]
                              pre_v["done"] = True
                      for f in nsched.get(idx, []):
                          f()
                      if vq and idx > len(steps) // 4 and (idx - len(steps) // 4) % max(1, (len(steps) * 5 // 8) // 8) == 0:
                          u = vq.pop(0)
                          u["pe_mm"]()
                          u["dve_red"]()
                      if idx < len(steps):
                          emit_S(steps[idx])
                      if idx >= SKEW:
                          emit_PV(steps[idx - SKEW])
                  while vq:
                      u = vq.pop(0)
                      u["pe_mm"]()
                      u["dve_red"]()
                  _chk('e%d' % G)
                  if DEFER_FG and not (G == NG - 1 and L == n_layers - 1):
                      pending_fg.extend(make_fg_phases(G, src_d, src_id, dst_d, dst_id, last))
                      continue
                  xq = []

                  def issue_xload(il_):
                      t_ = 4 * G + il_
                      xr_, xt_ = xar.next()
                      S.dma("sp", DMA(xt_[:], src_d[t_ * 128:(t_ + 1) * 128, :]), reads=[("xd", src_id, t_)], writes=xr_)
                      xq.append((xr_, xt_))
                  issue_xload(0)
                  issue_xload(1)
                  def og_unit(il):
                      tile = 4 * G + il
                      hold = {}

                      def s0():
                          tres, tp = tpr.next()
                          tpb = tp[:].bitcast(BF16)
                          hold["tp"] = (tres, tpb)
                          for k in range(8):
                              S.op("pe", TR(tpb[:, k * 128:(k + 1) * 128], Gt[:, il, k * 128:(k + 1) * 128], ident[:]),
                                   reads=[("Gt", il, 2 * k), ("Gt", il, 2 * k + 1), "ident"], writes=[tres], inc=(k == 7))

                      def s1():
                          tres, tpb = hold["tp"]
                          S.op("dve", CP(yT[:, :, il * 128:(il + 1) * 128], tpb.rearrange("p (k c) -> p k c", c=128)), reads=[tres], writes=[("yT", il)])

                      def s2():
                          hold["pj"] = [pj4.next(), pj4.next()]
                          for half in range(2):
                              pres, pj = hold["pj"][half]
                              for k in range(8):
                                  S.op("pe", MM(pj[:, 0:512], yT[:, k, il * 128:(il + 1) * 128], Wout[:, k, half * 512:(half + 1) * 512], k == 0, k == 7),
                                       reads=[("yT", il), ("Wout", k)], writes=[pres], inc=(k == 7))

                      def s3():
                          xres, xt = xq.pop(0)
                          for half in range(2):
                              pres, pj = hold["pj"][half]
                              S.op("dve", TT(xt[:, half * 512:(half + 1) * 512], pj[:, 0:512], xt[:, half * 512:(half + 1) * 512], ALU.add),
                                   reads=[pres] + xres, writes=xres)
                          S.dma("sp", DMA(dst_d[tile * 128:(tile + 1) * 128, :], xt[:]), reads=xres, writes=[("xd", dst_id, tile)], is_output=last)
                          if il + 2 < 4:
                              issue_xload(il + 2)
                      return (s0, s1, s2, s3)

                  run_skewed([og_unit(il) for il in range(4)])
        except _Stop:
            pass
        S.emit()
    return nc


def make_consts():
    kk = np.arange(128)[:, None]
    xx = np.arange(T)[None, :]
    d = xx - kk
    c = ((d >= 0) & (d <= 128)).astype(np.float32) + ((d >= 0) & (d <= 512) & (d % 4 == 0)).astype(np.float32) \
        + ((d >= 0) & (d % 16 == 0)).astype(np.float32)
    tri = (np.arange(128)[None, :] >= kk).astype(np.float32)
    eblk = np.zeros((128, 2, 8), np.float32)
    for n in range(8):
        eblk[n, 0, n] = 1.0
        eblk[64 + n, 1, n] = 1.0
    half = 32
    inv = np.float32(10000.0) ** (-(np.arange(half, dtype=np.float32) / np.float32(half)))
    ang = (np.arange(T, dtype=np.float32)[:, None] * inv[None, :].astype(np.float32)).astype(np.float32)
    cos = np.cos(ang.astype(np.float64)).astype(np.float32)
    sin = np.sin(ang.astype(np.float64)).astype(np.float32)
    cos2 = np.concatenate([cos, cos], axis=1)
    sin2 = np.concatenate([-sin, sin], axis=1)
    return dict(c_ident=np.eye(128, dtype=np.float32), c_mdil=np.ascontiguousarray(c[:, :1152]), c_tri=tri,
                c_eblk=eblk.reshape(128, 16), c_cos=np.ascontiguousarray(cos2), c_sin=np.ascontiguousarray(sin2))


_NC_CACHE = {}


def _get_nc(n_layers):
    if n_layers not in _NC_CACHE:
        _NC_CACHE[n_layers] = build(n_layers)
    return _NC_CACHE[n_layers]


FUSED = True


def kernel(x, mem, norm_g, w_in, w_out, mem_norm_g, w_mem_kv, qk_g):
    f = lambda a: np.ascontiguousarray(np.asarray(a, dtype=np.float32))
    x, mem, norm_g, w_in, w_out, mem_norm_g, w_mem_kv, qk_g = map(f, (x, mem, norm_g, w_in, w_out, mem_norm_g, w_mem_kv, qk_g))
    consts = make_consts()
    depth = w_in.shape[0]
    groups = [list(range(depth))] if FUSED else [[l] for l in range(depth)]
    cur = x
    for ls in groups:
        nl = len(ls)
        nc = _get_nc(nl)
        sl = slice(ls[0], ls[-1] + 1)
        in_maps = []
        for b in range(N_CORES):
            m = dict(x=cur[b], mem=mem[b], norm_g=norm_g[sl], w_in=w_in[sl], w_out=w_out[sl], mem_norm_g=mem_norm_g[sl],
                     w_mem_kv=w_mem_kv[sl], qk_g=qk_g[sl].reshape(nl, 384))
            m.update(consts)
            in_maps.append(m)
        res = run_bass_kernel_spmd(nc, in_maps, core_ids=list(range(N_CORES)))
        cur = np.stack([np.asarray(r["out"], dtype=np.float32) for r in res.results], axis=0)
    return cur
```

```python
import numpy as np
from contextlib import ExitStack
import concourse.bass as bass
import concourse.mybir as mybir
from concourse.bass_utils import run_bass_kernel_spmd

F32 = mybir.dt.float32
BF16 = mybir.dt.bfloat16
AF = mybir.ActivationFunctionType
ALU = mybir.AluOpType
AX = mybir.AxisListType

T = 2048
D = 1024
DIN = 3584
NMEM = 256
NT = T // 128
NG = T // 512
EPS = 1e-6
BIG = 30000.0
N_CORES = 8

SAME_ENGINE_SYNC = True
SKEW_ON = 3
PRE_A = True
DEFER_FG = True
ATT_SKEW = 7
INTERLEAVE_V = False
PRE_V = True


class Sched:
    ENGS = ("pe", "act", "dve", "pool", "sp")

    def __init__(self, nc, es, n_dma_sp=8, n_dma_pool=20, n_dma_act=2):
        self.nc = nc
        self.ops = {e: [] for e in self.ENGS}
        self.sem = {e: es.enter_context(nc.semaphore("s_" + e)) for e in self.ENGS}
        self.cnt = {e: 0 for e in self.ENGS}
        self.waited = {e: {} for e in self.ENGS}
        self.last_w = {}
        self.readers = {}
        self.semobj = {}
        for e in self.ENGS:
            self.semobj[("c", e)] = self.sem[e]
        self.dma_slots = {}
        self.dma_rr = {}
        for q, n in (("sp", n_dma_sp), ("pool", n_dma_pool), ("act", n_dma_act)):
            sl = []
            for i in range(n):
                key = ("d", q, i)
                self.semobj[key] = es.enter_context(nc.semaphore("d_%s%d" % (q, i)))
                sl.append([key, 0])
            self.dma_slots[q] = sl
            self.dma_rr[q] = 0
        self.out_waits = []

    def _deps(self, eng, reads, writes):
        deps = []
        for r in reads:
            if r in self.last_w:
                deps.append(self.last_w[r])
        for w in writes:
            if w in self.last_w:
                deps.append(self.last_w[w])
            deps.extend(self.readers.get(w, {}).items())
        wd = self.waited[eng]
        own = ("c", eng)
        m = {}
        for (key, val) in deps:
            if key == own and (eng == "pe" or not SAME_ENGINE_SYNC):
                continue
            if wd.get(key, 0) >= val:
                continue
            m[key] = max(m.get(key, 0), val)
        for k, v in m.items():
            wd[k] = v
        return list(m.items())

    def _commit(self, prod, reads, writes):
        for r in reads:
            d = self.readers.setdefault(r, {})
            d[prod[0]] = max(d.get(prod[0], 0), prod[1])
        for w in writes:
            self.last_w[w] = prod
            self.readers[w] = {}

    def op(self, eng, fn, reads=(), writes=(), inc=True):
        waits = self._deps(eng, reads, writes)
        if inc:
            self.cnt[eng] += 1
            prod = (("c", eng), self.cnt[eng])
            self.ops[eng].append((waits, fn, prod[0], 1))
        else:
            prod = (("c", eng), self.cnt[eng] + 1)
            self.ops[eng].append((waits, fn, prod[0], 0))
        self._commit(prod, reads, writes)
        return prod

    def dma(self, q, fn, reads=(), writes=(), is_output=False):
        sl = self.dma_slots[q]
        i = self.dma_rr[q]
        self.dma_rr[q] = (i + 1) % len(sl)
        key, tot = sl[i]
        waits = self._deps(q, reads, writes)
        wd = self.waited[q]
        if tot > 0 and wd.get(key, 0) < tot:
            wd[key] = tot
            waits = [(k, v) for (k, v) in waits if k != key] + [(key, tot)]
        sl[i][1] = tot + 16
        prod = (key, tot + 16)
        self.ops[q].append((waits, fn, key, 16))
        self._commit(prod, reads, writes)
        if is_output:
            self.out_waits.append(prod)
        return prod

    def emit(self):
        nc = self.nc
        m = {}
        for k, v in self.out_waits:
            m[k] = max(m.get(k, 0), v)
        final_waits = list(m.items())
        with nc.Block() as block:
            def run(engname, eng):
                for (waits, fn, key, inc) in self.ops[engname]:
                    for (k, v) in waits:
                        eng.wait_ge(self.semobj[k], v)
                    inst = fn(eng)
                    if inc:
                        inst.then_inc(self.semobj[key], inc)
                if engname == "sp":
                    for (k, v) in final_waits:
                        eng.wait_ge(self.semobj[k], v)

            @block.tensor
            def _(e):
                run("pe", e)

            @block.scalar
            def _(e):
                run("act", e)

            @block.vector
            def _(e):
                run("dve", e)

            @block.gpsimd
            def _(e):
                run("pool", e)

            @block.sync
            def _(e):
                run("sp", e)


class Rot:
    def __init__(self, items):
        self.items = items
        self.i = 0

    def next(self):
        it = self.items[self.i]
        self.i = (self.i + 1) % len(self.items)
        return it


def MM(out, lhsT, rhs, start, stop):
    return lambda e: e.matmul(out, lhsT=lhsT, rhs=rhs, start=start, stop=stop, skip_group_check=True)


def TR(out, in_, ident):
    return lambda e: e.transpose(out=out, in_=in_, identity=ident)


def ACT(out, in_, func, scale=None, accum_out=None, bias=None):
    kw = {}
    if bias is not None:
        kw["bias"] = bias
    if scale is not None:
        kw["scale"] = scale
    if accum_out is not None:
        kw["accum_out"] = accum_out
    return lambda e: e.activation(out=out, in_=in_, func=func, **kw)


def TT(out, in0, in1, op):
    return lambda e: e.tensor_tensor(out=out, in0=in0, in1=in1, op=op)


def TS(out, in0, s1, s2, op0, op1):
    return lambda e: e.tensor_scalar(out=out, in0=in0, scalar1=s1, scalar2=s2, op0=op0, op1=op1)


def STT(out, in0, scalar, in1, op0, op1):
    return lambda e: e.scalar_tensor_tensor(out=out, in0=in0, scalar=scalar, in1=in1, op0=op0, op1=op1)


def RED(out, in_, op):
    return lambda e: e.tensor_reduce(out=out, in_=in_, axis=AX.X, op=op)


def CP(out, in_):
    return lambda e: e.tensor_copy(out=out, in_=in_)


def MS(ap, v):
    return lambda e: e.memset(ap, v)


def RCP(out, in_):
    return lambda e: e.reciprocal(out=out, in_=in_)


def DMA(out, in_, **kw):
    return lambda e: e.dma_start(out=out, in_=in_, **kw)


SEGS = [
    ("a_q", 0, 384, "qk", dict(g=0, dst="Q", p0=0)),
    ("a_k", 384, 384, "qk", dict(g=1, dst="K", p0=0)),
    ("b_k", 1920, 384, "qk", dict(g=3, dst="K", p0=3)),
    ("m_q", 3072, 256, "qn", dict(g=4, dst="Q", p0=6)),
    ("b_q", 1536, 384, "qk", dict(g=2, dst="Q", p0=3)),
    ("a_v", 768, 384, "v", dict(h0=0)),
    ("b_v", 2304, 384, "v", dict(h0=6)),
    ("a_g", 1152, 384, "g", dict(h0=0)),
    ("b_g", 2688, 384, "g", dict(h0=6)),
    ("m_g", 3328, 256, "g", dict(h0=12)),
]


class _Stop(Exception):
    pass


STOP_AT = None


def _chk(tag):
    if STOP_AT is not None and tag == STOP_AT:
        raise _Stop()


def build(n_layers=2):
    nc = bass.Bass("TRN2", target_bir_lowering=False)

    def din(name, shape):
        return nc.dram_tensor(name, shape, F32, kind="ExternalInput").ap()

    x_d = din("x", [T, D])
    mem_d = din("mem", [NMEM, D])
    ng_d = din("norm_g", [n_layers, D])
    win_d = din("w_in", [n_layers, D, DIN])
    wout_d = din("w_out", [n_layers, D, D])
    mng_d = din("mem_norm_g", [n_layers, D])
    wmem_d = din("w_mem_kv", [n_layers, D, 512])
    qkg_d = din("qk_g", [n_layers, 384])
    cid_d = din("c_ident", [128, 128])
    cmd_d = din("c_mdil", [128, 1152])
    ctr_d = din("c_tri", [128, 128])
    ceb_d = din("c_eblk", [128, 16])
    ccos_d = din("c_cos", [T, 64])
    csin_d = din("c_sin", [T, 64])
    out_d = nc.dram_tensor("out", [T, D], F32, kind="ExternalOutput").ap()
    xmid_d = [nc.dram_tensor("xmid%d" % i, [T, D], F32, kind="Internal").ap() for i in range(n_layers - 1)]

    with ExitStack() as es:
        S = Sched(nc, es)

        def SB(name, shape, dt):
            return es.enter_context(nc.sbuf_tensor(name, shape, dt))

        Win = SB("Win", [128, 8, DIN], BF16)
        Wout = SB("Wout", [128, 8, D], BF16)
        yT = SB("yT", [128, 8, 512], BF16)
        hT = SB("hT", [128, 8, 512], BF16)
        KT = SB("KT", [128, 6, T], BF16)
        mKTs = [SB("mKT%d" % l, [128, 4, NMEM], BF16) for l in range(n_layers)]
        mkg = SB("mkg", [128, n_layers, 64], F32)
        Vc = SB("Vc", [128, NT, 12, 65], BF16)
        mVs = [SB("mV%d" % l, [128, 2, 4, 65], BF16) for l in range(n_layers)]
        ident = SB("ident", [128, 128], BF16)
        mdil = SB("mdil", [128, 1152], BF16)
        tri = SB("tri", [128, 128], BF16)
        eblk = SB("eblk", [128, 2, 8], BF16)
        cosb = SB("cosb", [128, NT, 64], F32)
        sinb = SB("sinb", [128, NT, 64], F32)
        gbuf = SB("gbuf", [128, D], F32)
        qkg = SB("qkg", [128, 6, 64], F32)
        mhalf = SB("mhalf", [128, 8], F32)
        xa = [SB("xa%d" % i, [128, D], F32) for i in range(2)]
        hbs = [SB("hb%d" % i, [128, D], BF16) for i in range(2)]
        QT = SB("QT", [128, 8, 512], BF16)
        Gt = SB("Gt", [128, 4, D], BF16)
        ptall = SB("ptall", [128, 8, 512], BF16)
        pts = [ptall[:, i, :] for i in range(8)]
        junk = ptall[:, 0:2, :].rearrange("p a c -> p (a c)")
        scrA_all = SB("scrA_all", [128, 2, 384], F32)
        scrA = [scrA_all[:, i, :] for i in range(2)]
        junk2 = scrA_all[:].rearrange("p a c -> p (a c)").bitcast(BF16)[:, 0:D]
        scrB = [SB("scrB%d" % i, [128, 384], F32) for i in range(2)]
        epsb = SB("epsb", [128, 1], F32)
        qktm = [SB("qktm%d" % i, [128, 384], BF16) for i in range(2)]
        biasT = SB("biasT", [128, 3, 512], BF16)
        kmT = SB("kmT", [128, 3, 8], BF16)
        ksum = SB("ksum", [128, 4], F32)
        gsb = SB("gsb", [128, 96], F32)
        rank = SB("rank", [128, 96], F32)
        bqs = [SB("bq%d" % i, [128, 96], BF16) for i in range(2)]
        st = SB("st", [128, 64], F32)
        rc = SB("rc", [128, 4], F32)
        print('SBUF free bytes/partition after alloc:', nc.sbuf_bytes_remaining)
        ps = [es.enter_context(nc.psum_tensor("ps%d" % i, [128, 512], F32)) for i in range(8)]

        pjr = Rot([("ps0", ps[0]), ("ps1", ps[1])])
        tpr = Rot([("ps2", ps[2]), ("ps3", ps[3])])
        str_ = Rot([("ps0", ps[0]), ("ps1", ps[1]), ("ps4", ps[4]), ("ps5", ps[5])])
        oar = Rot([("ps6", ps[6]), ("ps7", ps[7])])
        xar = Rot([(["xa0", "xC0", "xT0", "xT0a", "xT0b", "xH0a", "xH0b"], xa[0]), (["xa1", "xC1", "xT1", "xT1a", "xT1b", "xH1a", "xH1b"], xa[1])])
        ptr = Rot([("pt%d" % i, pts[i]) for i in range(8)])
        qkr = Rot([("qktm%d" % i, qktm[i]) for i in range(2)])
        scr = Rot([0, 1])
        bqr = Rot([("bq%d" % i, bqs[i]) for i in range(2)])
        scr3 = Rot([0, 1, 2])
        nsr = Rot([0, 1])
        hbr = Rot([("hb%d" % i, hbs[i]) for i in range(2)])

        S.dma("pool", DMA(ident[:], cid_d), writes=["ident"])
        S.dma("pool", DMA(mdil[:], cmd_d), writes=["mdil"])
        S.dma("pool", DMA(tri[:], ctr_d), writes=["tri"])
        S.dma("pool", DMA(eblk[:], ceb_d.rearrange("n (t a) -> n t a", t=2)), writes=["eblk"])
        S.dma("sp", DMA(cosb[:], ccos_d.rearrange("(t p) c -> p t c", p=128)), writes=["cos"])
        S.dma("sp", DMA(sinb[:], csin_d.rearrange("(t p) c -> p t c", p=128)), writes=["sin"])
        S.op("pool", MS(mhalf[:], -0.5), writes=["mhalf"])
        S.op("pool", MS(epsb[:], EPS), writes=["epsb"])
        S.op("pool", MS(QT[:], 0.0), writes=[("QT", h_, i_) for h_ in range(8) for i_ in range(4)])
        for l in range(n_layers):
            S.op("pool", MS(mKTs[l][:], 0.0), writes=[("mKT", l, 0), ("mKT", l, 1)])
            S.op("pool", MS(mVs[l][:, :, :, 64:65], 1.0), writes=[("mVones", l)])
            S.dma("sp", DMA(mkg[:, l, :], qkg_d[l:l + 1, 320:384].partition_broadcast(128)), writes=[("mkg", l)])
        S.op("pool", MS(kmT[:], 0.0), writes=[("kmT", n) for n in range(8)])
        S.op("pool", MS(biasT[:], 0.0), writes=[("biasT", i) for i in range(4)])
        S.op("pool", MS(Vc[:, :, :, 64:65], 1.0), writes=["Vones"])

        def h3(ap, nh):
            return ap.rearrange("p (h d) -> p h d", d=64) if len(ap.shape) == 2 else ap

        def norm_stages(src_ap, src_res, col0, ht_res, alt_junk=False):
            hold = {}

            def s0():
                xres, xt = xar.next()
                hres, hbt = hbr.next()
                sp = nsr.next()
                hold.update(x=(xres, xt), h=(hres, hbt), sp=sp)
                o = 56 + sp * 4
                S.dma("sp", DMA(xt[:], src_ap), reads=src_res, writes=xres)
                if alt_junk:
                    S.op("act", ACT(junk2, xt[:], AF.Square, accum_out=st[:, o:o + 1]), reads=xres, writes=["scrA0", "scrA1", "nss%d" % sp])
                else:
                    S.op("act", ACT(junk, xt[:], AF.Square, accum_out=st[:, o:o + 1]), reads=xres, writes=["pt0", "pt1", "nss%d" % sp])
                S.op("act", ACT(st[:, o + 1:o + 2], st[:, o:o + 1], AF.Ln, scale=1.0 / D, bias=epsb[:, 0:1]), reads=["nss%d" % sp, "epsb"], writes=["nlv%d" % sp])
                S.op("act", ACT(st[:, o + 2:o + 3], st[:, o + 1:o + 2], AF.Exp, scale=-0.5), reads=["nlv%d" % sp], writes=["nrs%d" % sp])

            def s1():
                xres, xt = hold["x"]
                hres, hbt = hold["h"]
                o = 56 + hold["sp"] * 4
                S.op("dve", STT(hbt[:], xt[:], st[:, o + 2:o + 3], gbuf[:], ALU.mult, ALU.mult), reads=xres + ["nrs%d" % hold["sp"], "gbuf"], writes=[hres])

            def s2():
                hres, hbt = hold["h"]
                tres, tp = tpr.next()
                tpb = tp[:].bitcast(BF16)
                for k in range(8):
                    S.op("pe", TR(tpb[:, k * 128:(k + 1) * 128], hbt[:, k * 128:(k + 1) * 128], ident[:]), reads=[hres, "ident"], writes=[tres], inc=(k == 7))
                S.op("act", ACT(hT[:, :, col0:col0 + 128], tpb.rearrange("p (k c) -> p k c", c=128), AF.Copy), reads=[tres], writes=ht_res)

            return (s0, s1, s2, None)

        def norm_parts(src_ap, src_res, col0, ht_res):
            hold = {}

            def dma():
                xres, xt = xar.next()
                hold["x"] = (xres, xt)
                S.dma("sp", DMA(xt[:], src_ap), reads=src_res, writes=xres)

            def act():
                xres, xt = hold["x"]
                hres, hbt = hbr.next()
                sp = nsr.next()
                hold.update(h=(hres, hbt), sp=sp)
                o = 56 + sp * 4
                S.op("dve", (lambda o_, x_, a_: (lambda e: e.scalar_tensor_tensor(out=o_, in0=x_, scalar=1.0, in1=x_, op0=ALU.mult, op1=ALU.mult, accum_out=a_)))(junk2, xt[:], st[:, o:o + 1]),
                     reads=xres, writes=["scrA0", "scrA1", "nss%d" % sp])
                S.op("act", ACT(st[:, o + 1:o + 2], st[:, o:o + 1], AF.Ln, scale=1.0 / D, bias=epsb[:, 0:1]), reads=["nss%d" % sp, "epsb"], writes=["nlv%d" % sp])
                S.op("act", ACT(st[:, o + 2:o + 3], st[:, o + 1:o + 2], AF.Exp, scale=-0.5), reads=["nlv%d" % sp], writes=["nrs%d" % sp])

            def dve():
                xres, xt = hold["x"]
                hres, hbt = hold["h"]
                o = 56 + hold["sp"] * 4
                S.op("dve", STT(hbt[:], xt[:], st[:, o + 2:o + 3], gbuf[:], ALU.mult, ALU.mult), reads=xres + ["nrs%d" % hold["sp"], "gbuf"], writes=[hres])

            def tr():
                hres, hbt = hold["h"]
                tres, tp = tpr.next()
                tpb = tp[:].bitcast(BF16)
                for k in range(8):
                    S.op("pe", TR(tpb[:, k * 128:(k + 1) * 128], hbt[:, k * 128:(k + 1) * 128], ident[:]), reads=[hres, "ident"], writes=[tres], inc=(k == 7))
                S.op("dve", CP(hT[:, :, col0:col0 + 128], tpb.rearrange("p (k c) -> p k c", c=128)), reads=[tres], writes=ht_res)

            return dict(dma=dma, act=act, dve=dve, tr=tr)

        def norm_tile(src_ap, src_res, col0, ht_res):
            a, b, c, _ = norm_stages(src_ap, src_res, col0, ht_res)
            a()
            b()
            c()

        def evac_qk_slots(ps_ap, ps_res, nh, gidx, rope_tile, dest_ap, dest_res):
            w = nh * 64
            sp = scr.next()
            A, B = scrA[sp], scrB[sp]
            rA, rB = "scrA%d" % sp, "scrB%d" % sp
            o = 8 + sp * 24
            ss, lv, rs = st[:, o:o + nh], st[:, o + 8:o + 8 + nh], st[:, o + 16:o + 16 + nh]
            rss, rlv, rrs = "ss%d" % sp, "lv%d" % sp, "rs%d" % sp
            ps3 = ps_ap.rearrange("p (h d) -> p h d", d=64)
            A3 = A[:, 0:w].rearrange("p (h d) -> p h d", d=64)
            B3 = B[:, 0:w].rearrange("p (h d) -> p h d", d=64)
            hold = {}
            Cb = xa[sp][:, 0:w].rearrange("p (h d) -> p h d", d=64)
            TA = xa[sp][:, 512:576]
            TB = xa[sp][:, 576:640]
            H3 = xa[sp][:, 640:640 + w].rearrange("p (h d) -> p h d", d=64)
            rC, rT, rH = "xC%d" % sp, "xT%d" % sp, "xH%d" % sp
            if isinstance(gidx, tuple):
                gv, gres = mkg[:, gidx[1], :], ("mkg", gidx[1])
            else:
                gv, gres = qkg[:, gidx, :], "qkg"
            rope = rope_tile is not None
            sl = {}

            def act_sq():
                S.op("act", ACT(A[:, 0:w], ps_ap, AF.Square), reads=[ps_res], writes=[rA])
            sl["act_sq"] = act_sq

            def dve_red():
                S.op("dve", RED(ss, A3, ALU.add), reads=[rA], writes=[rss])
            sl["dve_red"] = dve_red

            def act_lnexp():
                S.op("act", ACT(lv, ss, AF.Ln, scale=1.0 / 64, bias=epsb[:, 0:1]), reads=[rss, "epsb"], writes=[rlv])
                S.op("act", ACT(rs, lv, AF.Exp, scale=-0.5), reads=[rlv], writes=[rrs])
            sl["act_lnexp"] = act_lnexp

            def dve_tab():
                S.op("pool", TT(TA, cosb[:, rope_tile, :], gv, ALU.mult), reads=["cos", gres], writes=[rT])
                S.op("pool", TT(TB[:, 0:32], sinb[:, rope_tile, 0:32], gv[:, 32:64], ALU.mult), reads=["sin", gres], writes=[rT + "a"])
                S.op("pool", TT(TB[:, 32:64], sinb[:, rope_tile, 32:64], gv[:, 0:32], ALU.mult), reads=["sin", gres], writes=[rT + "b"])
            if rope:
                sl["dve_tab"] = dve_tab

            def dve_BC():
                S.op("dve", TT(B3, ps3, rs.unsqueeze(2).broadcast_to([128, nh, 64]), ALU.mult), reads=[ps_res, rrs], writes=[rB])
                qres, qb = qkr.next()
                hold["q"] = (qres, qb)
                qb3 = qb[:, 0:w].rearrange("p (h d) -> p h d", d=64)
                if not rope:
                    pass
                else:
                    S.op("dve", TT(Cb, B3, TA.unsqueeze(1).broadcast_to([128, nh, 64]), ALU.mult), reads=[rB, rT], writes=[rC])
            sl["dve_BC"] = dve_BC

            def pool_halves():
                H4 = xa[sp][:, 640:640 + w].rearrange("p (h j d) -> p h j d", j=2, d=32)
                Br = B[:, 0:w].rearrange("p (h j d) -> p h j d", j=2, d=32)[:, :, ::-1, :]
                TB4 = TB.rearrange("p (j d) -> p j d", d=32).unsqueeze(1).broadcast_to([128, nh, 2, 32])
                S.op("pool", TT(H4, Br, TB4, ALU.mult), reads=[rB, rT + "a", rT + "b"], writes=[rH + "a", rH + "b"])
            if rope:
                sl["pool_halves"] = pool_halves

            def dve_add():
                qres, qb = hold["q"]
                qb3 = qb[:, 0:w].rearrange("p (h d) -> p h d", d=64)
                if rope:
                    S.op("dve", TT(qb3, Cb, H3, ALU.add), reads=[rC, rH + "a", rH + "b"], writes=[qres])
                else:
                    S.op("dve", TT(qb3, B3, gv.unsqueeze(1).broadcast_to([128, nh, 64]), ALU.mult), reads=[rB, gres], writes=[qres])
            sl["dve_add"] = dve_add

            def pe_tr():
                qres, qb = hold["q"]
                tres, tp = tpr.next()
                tpb = tp[:].bitcast(BF16)
                hold["tp"] = (tres, tpb)
                for pi in range(nh // 2):
                    S.op("pe", TR(tpb[:, pi * 128:(pi + 1) * 128], qb[:, pi * 128:(pi + 1) * 128], ident[:]), reads=[qres, "ident"], writes=[tres], inc=(pi == nh // 2 - 1))
            sl["pe_tr"] = pe_tr

            def act_copy():
                tres, tpb = hold["tp"]
                if callable(dest_ap):
                    dest_ap(tpb, tres)
                else:
                    S.op("act", ACT(dest_ap, tpb[:, 0:(nh // 2) * 128].rearrange("p (a c) -> p a c", c=128), AF.Copy), reads=[tres], writes=dest_res)
            sl["act_copy"] = act_copy
            return sl

        SLOT_SCHED = [("dve_add", 6), ("act_sq", 1), ("pe_tr", 7), ("dve_red", 2), ("act_lnexp", 3),
                      ("dve_BC", 4), ("pool_halves", 4), ("dve_tab", 3), ("act_copy", 8), ("pe_mm", 0)]
        SLOT_DEPTH = 8
        SLOT_LOGICAL = ["pe_mm", "act_sq", "dve_red", "act_lnexp", "dve_tab", "dve_BC", "pool_halves", "dve_add", "pe_tr", "act_copy"]

        def run_slots(units, extras=None):
            n = len(units)
            extras = extras or {}
            if not (SKEW_ON & 2):
                for u in units:
                    for nm in SLOT_LOGICAL:
                        if nm in u:
                            u[nm]()
                for t in sorted(extras):
                    for f in extras[t]:
                        f()
                return
            for t in range(max(n + SLOT_DEPTH, max(extras) + 1 if extras else 0)):
                for (nm, k) in SLOT_SCHED:
                    u = t - k
                    if 0 <= u < n and nm in units[u]:
                        units[u][nm]()
                for f in extras.get(t, []):
                    f()

        def evac_qk(*a):
            sl = evac_qk_slots(*a)
            for nm in SLOT_LOGICAL:
                if nm in sl:
                    sl[nm]()

        def run_skewed(units, skew=1):
            n = len(units)
            ns = max(len(u) for u in units)
            if not (SKEW_ON & skew):
                for u in units:
                    for f in u:
                        if f is not None:
                            f()
                return
            for t in range(n + ns - 1):
                for k in range(ns - 1, -1, -1):
                    u = t - k
                    if 0 <= u < n and k < len(units[u]) and units[u][k] is not None:
                        units[u][k]()

        pj4 = Rot([("ps0", ps[0]), ("ps1", ps[1]), ("ps4", ps[4]), ("ps5", ps[5]), ("ps6", ps[6])])

        def make_unit(G, il, c0, w, kind, ex, bankrot=None):
            tile = 4 * G + il
            hold = {}
            u = {}

            def pe_mm():
                pres, pj = (bankrot or pj4).next()
                hold["pj"] = (pres, pj)
                for k in range(8):
                    S.op("pe", MM(pj[:, 0:w], hT[:, k, il * 128:(il + 1) * 128], Win[:, k, c0:c0 + w], k == 0, k == 7),
                         reads=[("hT", il), ("Win", c0)], writes=[pres], inc=(k == 7))
            u["pe_mm"] = pe_mm

            if kind in ("qk", "qn"):
                nh = w // 64
                npair = nh // 2
                p0 = ex["p0"]
                if ex["dst"] == "Q" and kind == "qk":
                    base = QT if p0 == 0 else yT
                    zero_too = (p0 != 0)
                    dres = [("QT", i, il) for i in range(6)] if p0 == 0 else [("yT", il)]
                    bv = base[:, 0:6, il * 128:(il + 1) * 128].rearrange("p (a two) c -> p a two c", two=2)

                    def dest(tpb, tres, bv=bv, dres=dres, zero_too=zero_too):
                        src = tpb[:, 0:384].rearrange("p (a c) -> p a c", c=128)
                        for par in range(2):
                            rows = slice(par * 64, par * 64 + 64)
                            S.op("act", ACT(bv[rows, :, par, :], src[rows], AF.Copy), reads=[tres], writes=dres)
                            if zero_too:
                                orow = slice((1 - par) * 64, (1 - par) * 64 + 64)
                                S.op("act", ACT(bv[orow, :, par, :], src[orow], AF.Copy, scale=0.0), reads=[tres], writes=dres)
                elif ex["dst"] == "Q":
                    dest = QT[:, p0:p0 + npair, il * 128:(il + 1) * 128]
                    dres = [("QT", p0 + i, il) for i in range(npair)]
                else:
                    dest = KT[:, p0:p0 + npair, tile * 128:(tile + 1) * 128]
                    dres = [("KT", p0 + i, tile) for i in range(npair)]

                def lazy(nm):
                    def f():
                        if "sl" not in hold:
                            pres, pj = hold["pj"]
                            hold["sl"] = evac_qk_slots(pj[:, 0:w], pres, nh, ex["g"], tile if kind == "qk" else None, dest, dres)
                        if nm in hold["sl"]:
                            hold["sl"][nm]()
                    return f
                for nm in SLOT_LOGICAL[1:]:
                    if kind == "qn" and nm in ("dve_tab", "pool_halves"):
                        continue
                    u[nm] = lazy(nm)
            elif kind == "v":
                h0 = ex["h0"]

                def vcopy():
                    pres, pj = hold["pj"]
                    if True:
                        S.op("act", ACT(Vc[:, tile, h0:h0 + 6, 0:64], pj[:, 0:w].rearrange("p (h d) -> p h d", d=64), AF.Copy),
                             reads=[pres, "Vones"], writes=[("V", tile, h0 // 6)])
                    else:
                        S.op("dve", CP(Vc[:, tile, h0:h0 + 6, 0:64], pj[:, 0:w].rearrange("p (h d) -> p h d", d=64)),
                             reads=[pres, "Vones"], writes=[("V", tile, h0 // 6)])
                u["dve_red"] = vcopy
            else:
                h0 = ex["h0"]

                def gsilu():
                    pres, pj = hold["pj"]
                    S.op("act", ACT(Gt[:, il, h0 * 64:h0 * 64 + w], pj[:, 0:w], AF.Silu), reads=[pres],
                         writes=[("Gt", il, h0 + i) for i in range(w // 64)])
                u["act_sq"] = gsilu
            return u

        for L in range(n_layers):
            S.dma("pool", DMA(yT[:], wmem_d[L].rearrange("(k p) c -> p k c", p=128)), writes=[("yT", i) for i in range(4)])
            if L == n_layers - 1:
                for (sname, c0, w, kind, ex) in SEGS:
                    S.dma("pool", DMA(Win[:, :, c0:c0 + w], win_d[0][:, c0:c0 + w].rearrange("(k p) c -> p k c", p=128)), writes=[("Win", c0)])
            S.dma("sp", DMA(gbuf[:], mng_d[L:L + 1, :].partition_broadcast(128)), writes=["gbuf"])
            def mem_unit(mt, L=L):
                n0, n1, n2, _ = norm_stages(mem_d[mt * 128:(mt + 1) * 128, :], [], mt * 128, [("hT", mt)])
                hold = {}

                def mm():
                    pres, pj = pjr.next()
                    hold["pj"] = (pres, pj)
                    for k in range(8):
                        S.op("pe", MM(pj[:, 0:512], hT[:, k, mt * 128:(mt + 1) * 128], yT[:, k, :], k == 0, k == 7),
                             reads=[("hT", mt)] + [("yT", i) for i in range(4)], writes=[pres], inc=(k == 7))

                mkv = mKTs[L][:, :, mt * 128:(mt + 1) * 128].rearrange("p (a two) c -> p a two c", two=2)

                def mdest(tpb, tres):
                    src = tpb[:, 0:256].rearrange("p (a c) -> p a c", c=128)
                    for par in range(2):
                        rows = slice(par * 64, par * 64 + 64)
                        S.op("act", ACT(mkv[rows, :, par, :], src[rows], AF.Copy), reads=[tres], writes=[("mKT", L, mt)])

                def ev(names, extra=None):
                    def f():
                        if "sl" not in hold:
                            pres, pj = hold["pj"]
                            hold["sl"] = evac_qk_slots(pj[:, 0:256], pres, 4, ("mkg", L), None, mdest, [("mKT", L, mt)])
                        for nm in names:
                            if nm in hold["sl"]:
                                hold["sl"][nm]()
                        if extra:
                            extra()
                    return f

                def vcopy():
                    pres, pj = hold["pj"]
                    S.op("dve", CP(mVs[L][:, mt, :, 0:64], pj[:, 256:512].rearrange("p (h d) -> p h d", d=64)), reads=[pres, ("mVones", L)], writes=[("mV", L, mt)])

                return (n0, n1, n2, mm, ev(["act_sq", "dve_red"], vcopy), ev(["act_lnexp"]), ev(["dve_BC"]), ev(["dve_add"]), ev(["pe_tr"]), ev(["act_copy"]))

            run_skewed([mem_unit(0), mem_unit(1)])

        def make_fg_phases(Gp, src_p, src_id_p, dst_p, dst_id_p, last_p):
            phases = []
            tpb7 = ps[7][:].bitcast(BF16)

            def xload(il, half):
                t_ = 4 * Gp + il
                S.dma("sp", DMA(hbs[half][:].bitcast(F32), src_p[t_ * 128:(t_ + 1) * 128, half * 512:(half + 1) * 512]),
                      reads=[("xd", src_id_p, t_)], writes=["hb%d" % half])

            for il in range(4):
                tile = 4 * Gp + il

                def p_tr(il=il):
                    if il == 0:
                        xload(0, 0)
                        xload(0, 1)
                    for k in range(8):
                        S.op("pe", TR(tpb7[:, k * 128:(k + 1) * 128], Gt[:, il, k * 128:(k + 1) * 128], ident[:]),
                             reads=[("Gt", il, 2 * k), ("Gt", il, 2 * k + 1), "ident"], writes=["ps7"], inc=(k == 7))

                def p_cp(il=il):
                    S.op("act", ACT(yT[:, :, il * 128:(il + 1) * 128], tpb7.rearrange("p (k c) -> p k c", c=128), AF.Copy), reads=["ps7"], writes=[("yT", il)])

                def p_mm(half, il=il):
                    for k in range(8):
                        S.op("pe", MM(ps[7][:, 0:512], yT[:, k, il * 128:(il + 1) * 128], Wout[:, k, half * 512:(half + 1) * 512], k == 0, k == 7),
                             reads=[("yT", il), ("Wout", k)], writes=["ps7"], inc=(k == 7))

                def p_add(half, il=il, tile=tile):
                    xt = hbs[half][:].bitcast(F32)
                    S.op("dve", TT(xt, ps[7][:, 0:512], xt, ALU.add), reads=["ps7", "hb%d" % half], writes=["hb%d" % half])
                    S.dma("sp", DMA(dst_p[tile * 128:(tile + 1) * 128, half * 512:(half + 1) * 512], xt), reads=["hb%d" % half],
                          writes=[("xd", dst_id_p, tile)], is_output=last_p)
                    if il + 1 < 4:
                        xload(il + 1, half)

                phases += [p_tr, p_cp, (lambda il=il: p_mm(0, il)), (lambda il=il, tile=tile: p_add(0, il, tile)),
                           (lambda il=il: p_mm(1, il)), (lambda il=il, tile=tile: p_add(1, il, tile))]
            return phases

        pending_fg = []
        pre_a = {"done": False}
        pre_v = {"done": False}
        ps1r = Rot([("ps1", ps[1])])
        try:
          for L in range(n_layers):
              src_d = x_d if L == 0 else xmid_d[L - 1]
              src_id = "x" if L == 0 else "mid%d" % (L - 1)
              last = (L == n_layers - 1)
              dst_d = out_d if last else xmid_d[L]
              dst_id = "out" if last else "mid%d" % L

              if L > 0:
                  for (sname, c0, w, kind, ex) in SEGS:
                      S.dma("pool", DMA(Win[:, :, c0:c0 + w], win_d[L][:, c0:c0 + w].rearrange("(k p) c -> p k c", p=128)), writes=[("Win", c0)])
              if not pre_a["done"]:
                  S.dma("sp", DMA(gbuf[:], ng_d[L:L + 1, :].partition_broadcast(128)), writes=["gbuf"])
              S.dma("sp", DMA(qkg[:].rearrange("p a d -> p (a d)"), qkg_d[L:L + 1, :].partition_broadcast(128)), writes=["qkg"])
              mKT, mV = mKTs[L], mVs[L]

              _chk('mem')
              for G in range(NG):
                  if not pre_a["done"]:
                      run_skewed([norm_stages(src_d[(4 * G + il) * 128:(4 * G + il + 1) * 128, :], [("xd", src_id, 4 * G + il)], il * 128, [("hT", il)])
                                  for il in range(4)])
                  pre_a["done"] = False
                  _chk('a%d' % G)
                  uq, uv, ug = [], [], []
                  for (sname, c0, w, kind, ex) in SEGS:
                      if kind == "v" and pre_v["done"]:
                          continue
                      for il in range(4):
                          u = make_unit(G, il, c0, w, kind, ex)
                          (uq if kind in ("qk", "qn") else uv if kind == "v" else ug).append(u)
                  pre_v["done"] = False
                  units = []
                  for i, u in enumerate(uq):
                      units.append(u)
                      if INTERLEAVE_V and i % 2 == 1 and uv:
                          units.append(uv.pop(0))
                  units += uv + ug

                  def kmean_ops():
                      for n in (2 * G, 2 * G + 1):
                          S.op("dve", RED(ksum[:, 0:3], KT[:, 0:3, n * 256:(n + 1) * 256], ALU.add),
                               reads=[("KT", p, t) for p in range(3) for t in (2 * n, 2 * n + 1)], writes=["ksum"])
                          S.op("dve",
                               (lambda o, i: (lambda e: e.tensor_scalar_mul(out=o, in0=i, scalar1=1.0 / 256)))(kmT[:, :, n], ksum[:, 0:3]),
                               reads=["ksum"], writes=[("kmT", n)])
                  extras = {25: [kmean_ops]}
                  for i_, f_ in enumerate(pending_fg):
                      extras.setdefault(1 + i_, []).append(f_)
                  del pending_fg[:]
                  gated_group = (2 * G + 1) > 3
                  if gated_group:
                      def gate_unit(b):
                          c = (4 * G + 2 * b) // 2
                          hold = {}
                          g4 = gsb[:].rearrange("p (t s n) -> p t s n", t=2, n=8)
                          g3 = gsb[:].rearrange("p (a n) -> p a n", n=8)

                          def s0():
                              for ti in range(2):
                                  il = 2 * b + ti
                                  for h in range(6):
                                      col = ti * 48 + h * 8
                                      S.op("pe", MM(ps[7][:, col:col + 8], QT[:, h, il * 128:(il + 1) * 128], kmT[:, h // 2, :], True, True),
                                           reads=[("QT", h, il)] + [("kmT", n) for n in range(8)], writes=["ps7"])

                          def s1():
                              hold["bq"] = bqr.next()
                              bres_, bq = hold["bq"]
                              S.op("dve", CP(gsb[:], ps[7][:, 0:96]), reads=["ps7"], writes=["gsb"])
                              S.op("dve", MS(g3[:, :, c:8], -1e30), reads=[], writes=["gsb"])
                              for ti in range(2):
                                  cmf = hbs[ti][:].bitcast(F32)[:, 0:384]
                                  gt = g3[:, ti * 6:(ti + 1) * 6, :]
                                  S.op("dve", TT(cmf.rearrange("p (a n m) -> p a n m", n=8, m=8), gt.unsqueeze(2).broadcast_to([128, 6, 8, 8]),
                                                 gt.unsqueeze(3).broadcast_to([128, 6, 8, 8]), ALU.is_gt), reads=["gsb"], writes=["hb%d" % ti])
                                  S.op("dve", RED(rank[:, ti * 48:(ti + 1) * 48], cmf.rearrange("p (a m) -> p a m", m=8), ALU.add),
                                       reads=["hb%d" % ti], writes=["rank"])
                              S.op("dve", TS(bq[:], rank[:], 3.0, -BIG, ALU.is_ge, ALU.mult), reads=["rank"], writes=[bres_])
                              b3 = bq[:].rearrange("p (a n) -> p a n", n=8)
                              S.op("dve", MS(b3[:, :, c:c + 1], 0.0), reads=[], writes=[bres_])

                          def s2():
                              bres_, bq = hold["bq"]
                              b3 = bq[:].rearrange("p (a n) -> p a n", n=8)
                              tpb2 = ps[7][:].bitcast(BF16)
                              for ti in range(2):
                                  for h in range(6):
                                      par, pr = h % 2, h // 2
                                      S.op("pe", TR(tpb2[par * 64:par * 64 + 8, (ti * 3 + pr) * 128:(ti * 3 + pr + 1) * 128], b3[:, ti * 6 + h, :], ident[:]),
                                           reads=[bres_, "ident"], writes=["ps7"])

                          def s3():
                              tpb2 = ps[7][:].bitcast(BF16)
                              for ti in range(2):
                                  il = 2 * b + ti
                                  for par in range(2):
                                      S.op("dve", CP(biasT[par * 64:par * 64 + 8, :, il * 128:(il + 1) * 128],
                                                     tpb2[par * 64:par * 64 + 8, ti * 384:(ti + 1) * 384].rearrange("p (h c) -> p h c", c=128)),
                                           reads=["ps7"], writes=[("biasT", il)])
                          return (s0, s1, s2, s3)

                      g0, g1 = gate_unit(0), gate_unit(1)
                      for i_, f_ in enumerate(list(g0) + list(g1)):
                          extras.setdefault(27 + 2 * i_, []).append(f_)
                  run_slots(units, extras)
                  if G == 0:
                      for k in range(8):
                          S.dma("pool", DMA(Wout[:, k, :], wout_d[L, k * 128:(k + 1) * 128, :]), writes=[("Wout", k)])
                  _chk('d%d' % G)
                  steps = []
                  for h in range(16):
                      if h < 12:
                          kind = "A" if h < 6 else "B"
                          pair = h // 2
                          nkt = 4 * G + 4
                      else:
                          kind = "M"
                          pair = 6 + (h - 12) // 2
                          nkt = 2
                      for j in range(nkt):
                          steps.append(dict(h=h, kind=kind, pair=pair, nkt=nkt, j=j))
                  SKEW = ATT_SKEW
                  cur_o = {}

                  def emit_S(sd):
                      h, kind, pair, j = sd["h"], sd["kind"], sd["pair"], sd["j"]
                      r0 = (h % 2) * 64
                      qs = 512 * G if kind == "M" else max(512 * G, 128 * j)
                      N = 512 * (G + 1) - qs
                      ql = qs - 512 * G
                      sres, stp = str_.next()
                      il_lo = ql // 128
                      if kind == "M":
                          lhsT = mKT[:, h - 12, j * 128:(j + 1) * 128]
                          kreads = [("mKT", L, j)]
                          rhsq = QT[:, pair, ql:ql + N]
                          qreads = [("QT", pair, i) for i in range(il_lo, 4)]
                      elif kind == "A":
                          lhsT = KT[:, pair, j * 128:(j + 1) * 128]
                          kreads = [("KT", pair, j)]
                          rhsq = QT[:, h, ql:ql + N]
                          qreads = [("QT", h, i) for i in range(il_lo, 4)]
                      else:
                          lhsT = KT[:, pair, j * 128:(j + 1) * 128]
                          kreads = [("KT", pair, j)]
                          rhsq = yT[:, h - 6, ql:ql + N]
                          qreads = [("yT", i) for i in range(il_lo, 4)]
                      n = j // 2
                      gated = (kind == "A") and gated_group and (n <= 2 * G)
                      S.op("pe", MM(stp[:, 0:N], lhsT, rhsq, True, not gated), reads=kreads + qreads, writes=[sres], inc=(not gated))
                      if gated:
                          S.op("pe", MM(stp[:, 0:N], eblk[:, h % 2, n:n + 1].broadcast_to([128, 128]), biasT[:, pair, ql:ql + N], False, True),
                               reads=["eblk"] + [("biasT", i) for i in range(il_lo, 4)], writes=[sres])
                      pres_, pt = ptr.next()
                      S.op("act", ACT(pt[:, 0:N], stp[:, 0:N], AF.Exp, scale=0.125), reads=[sres], writes=[pres_])
                      if kind == "A" and j >= 4 * G:
                          S.op("dve", TT(pt[:, 0:128], pt[:, 0:128], tri[:], ALU.mult), reads=[pres_, "tri"], writes=[pres_])
                      if kind == "B":
                          x0 = min(qs - 128 * j, 640)
                          S.op("dve", TT(pt[:, 0:N], pt[:, 0:N], mdil[:, x0:x0 + N], ALU.mult), reads=[pres_, "mdil"], writes=[pres_])
                      sd["pt"] = (pres_, pt)
                      sd["qs"] = qs

                  def emit_PV(sd):
                      h, kind, j, nkt, qs = sd["h"], sd["kind"], sd["j"], sd["nkt"], sd["qs"]
                      pres_, pt = sd["pt"]
                      if j == 0:
                          cur_o["o"] = oar.next()
                      ores, oa = cur_o["o"]
                      for i in range(qs // 128, 4 * G + 4):
                          il = i - 4 * G
                          c0 = i * 128 - qs
                          if kind == "M":
                              rhs = mV[:, j, h - 12, :]
                              vreads = [("mV", L, j), ("mVones", L)]
                          else:
                              rhs = Vc[:, j, h, :]
                              vreads = [("V", j, h // 6), "Vones"]
                          S.op("pe", MM(oa[:, il * 128:il * 128 + 65], pt[:, c0:c0 + 128], rhs, (j == 0 and i == qs // 128), j == nkt - 1),
                               reads=[pres_] + vreads, writes=[ores], inc=(i == 4 * G + 3))
                      if j == nkt - 1:
                          oa3 = oa[:].rearrange("p (a c) -> p a c", c=128)
                          S.op("dve", RCP(rc[:, 0:4], oa3[:, :, 64]), reads=[ores], writes=["rc"])
                          for il in range(4):
                              gap = Gt[:, il, h * 64:(h + 1) * 64]
                              S.op("dve", STT(gap, oa[:, il * 128:il * 128 + 64], rc[:, il:il + 1], gap, ALU.mult, ALU.mult),
                                   reads=[ores, "rc", ("Gt", il, h)], writes=[("Gt", il, h)])

                  vq = []
                  nsched = {}
                  for idx in range(len(steps) + SKEW):
                      if PRE_A and idx == 0 and not (G == NG - 1 and L == n_layers - 1):
                          if G < NG - 1:
                              nsrc, nsid, nG = src_d, src_id, G + 1
                          else:
                              nsrc, nsid, nG = dst_d, dst_id, 0
                              S.dma("sp", DMA(gbuf[:], ng_d[L + 1:L + 2, :].partition_broadcast(128)), writes=["gbuf"])
                          nparts = [norm_parts(nsrc[(4 * nG + il) * 128:(4 * nG + il + 1) * 128, :], [("xd", nsid, 4 * nG + il)], il * 128, [("hT", il)])
                                    for il in range(4)]
                          nsched = {}
                          for il in range(4):
                              base = (il + 1) * (6 * (4 * G + 4)) // 5
                              for off, nm in ((-5, "dma"), (0, "act"), (1, "dve"), (3, "tr")):
                                  nsched.setdefault(max(0, base + off), []).append(nparts[il][nm])
                          pre_a["done"] = True
                          if PRE_V and G < NG - 1 and False:
                              vq = [make_unit(G + 1, il, c0, w, kind, ex, bankrot=ps1r) for (sname, c0, w, kind, ex) in SEGS if kind == "v" for il in range(4)]
                              pre_v["done"] = True
                      for f in nsched.get(idx, []):
                          f()
                      if vq and idx > len(steps) // 4 and (idx - len(steps) // 4) % max(1, (len(steps) * 5 // 8) // 8) == 0:
                          u = vq.pop(0)
                          u["pe_mm"]()
                          u["dve_red"]()
                      if idx < len(steps):
                          emit_S(steps[idx])
                      if idx >= SKEW:
                          emit_PV(steps[idx - SKEW])
                  while vq:
                      u = vq.pop(0)
                      u["pe_mm"]()
                      u["dve_red"]()
                  _chk('e%d' % G)
                  if DEFER_FG and not (G == NG - 1 and L == n_layers - 1):
                      pending_fg.extend(make_fg_phases(G, src_d, src_id, dst_d, dst_id, last))
                      continue
                  xq = []

                  def issue_xload(il_):
                      t_ = 4 * G + il_
                      xr_, xt_ = xar.next()
                      S.dma("sp", DMA(xt_[:], src_d[t_ * 128:(t_ + 1) * 128, :]), reads=[("xd", src_id, t_)], writes=xr_)
                      xq.append((xr_, xt_))
                  issue_xload(0)
                  issue_xload(1)
                  def og_unit(il):
                      tile = 4 * G + il
                      hold = {}

                      def s0():
                          tres, tp = tpr.next()
                          tpb = tp[:].bitcast(BF16)
                          hold["tp"] = (tres, tpb)
                          for k in range(8):
                              S.op("pe", TR(tpb[:, k * 128:(k + 1) * 128], Gt[:, il, k * 128:(k + 1) * 128], ident[:]),
                                   reads=[("Gt", il, 2 * k), ("Gt", il, 2 * k + 1), "ident"], writes=[tres], inc=(k == 7))

                      def s1():
                          tres, tpb = hold["tp"]
                          S.op("dve", CP(yT[:, :, il * 128:(il + 1) * 128], tpb.rearrange("p (k c) -> p k c", c=128)), reads=[tres], writes=[("yT", il)])

                      def s2():
                          hold["pj"] = [pj4.next(), pj4.next()]
                          for half in range(2):
                              pres, pj = hold["pj"][half]
                              for k in range(8):
                                  S.op("pe", MM(pj[:, 0:512], yT[:, k, il * 128:(il + 1) * 128], Wout[:, k, half * 512:(half + 1) * 512], k == 0, k == 7),
                                       reads=[("yT", il), ("Wout", k)], writes=[pres], inc=(k == 7))

                      def s3():
                          xres, xt = xq.pop(0)
                          for half in range(2):
                              pres, pj = hold["pj"][half]
                              S.op("dve", TT(xt[:, half * 512:(half + 1) * 512], pj[:, 0:512], xt[:, half * 512:(half + 1) * 512], ALU.add),
                                   reads=[pres] + xres, writes=xres)
                          S.dma("sp", DMA(dst_d[tile * 128:(tile + 1) * 128, :], xt[:]), reads=xres, writes=[("xd", dst_id, tile)], is_output=last)
                          if il + 2 < 4:
                              issue_xload(il + 2)
                      return (s0, s1, s2, s3)

                  run_skewed([og_unit(il) for il in range(4)])
        except _Stop:
            pass
        S.emit()
    return nc


def make_consts():
    kk = np.arange(128)[:, None]
    xx = np.arange(T)[None, :]
    d = xx - kk
    c = ((d >= 0) & (d <= 128)).astype(np.float32) + ((d >= 0) & (d <= 512) & (d % 4 == 0)).astype(np.float32) \
        + ((d >= 0) & (d % 16 == 0)).astype(np.float32)
    tri = (np.arange(128)[None, :] >= kk).astype(np.float32)
    eblk = np.zeros((128, 2, 8), np.float32)
    for n in range(8):
        eblk[n, 0, n] = 1.0
        eblk[64 + n, 1, n] = 1.0
    half = 32
    inv = np.float32(10000.0) ** (-(np.arange(half, dtype=np.float32) / np.float32(half)))
    ang = (np.arange(T, dtype=np.float32)[:, None] * inv[None, :].astype(np.float32)).astype(np.float32)
    cos = np.cos(ang.astype(np.float64)).astype(np.float32)
    sin = np.sin(ang.astype(np.float64)).astype(np.float32)
    cos2 = np.concatenate([cos, cos], axis=1)
    sin2 = np.concatenate([-sin, sin], axis=1)
    return dict(c_ident=np.eye(128, dtype=np.float32), c_mdil=np.ascontiguousarray(c[:, :1152]), c_tri=tri,
                c_eblk=eblk.reshape(128, 16), c_cos=np.ascontiguousarray(cos2), c_sin=np.ascontiguousarray(sin2))


_NC_CACHE = {}


def _get_nc(n_layers):
    if n_layers not in _NC_CACHE:
        _NC_CACHE[n_layers] = build(n_layers)
    return _NC_CACHE[n_layers]


FUSED = True


def kernel(x, mem, norm_g, w_in, w_out, mem_norm_g, w_mem_kv, qk_g):
    f = lambda a: np.ascontiguousarray(np.asarray(a, dtype=np.float32))
    x, mem, norm_g, w_in, w_out, mem_norm_g, w_mem_kv, qk_g = map(f, (x, mem, norm_g, w_in, w_out, mem_norm_g, w_mem_kv, qk_g))
    consts = make_consts()
    depth = w_in.shape[0]
    groups = [list(range(depth))] if FUSED else [[l] for l in range(depth)]
    cur = x
    for ls in groups:
        nl = len(ls)
        nc = _get_nc(nl)
        sl = slice(ls[0], ls[-1] + 1)
        in_maps = []
        for b in range(N_CORES):
            m = dict(x=cur[b], mem=mem[b], norm_g=norm_g[sl], w_in=w_in[sl], w_out=w_out[sl], mem_norm_g=mem_norm_g[sl],
                     w_mem_kv=w_mem_kv[sl], qk_g=qk_g[sl].reshape(nl, 384))
            m.update(consts)
            in_maps.append(m)
        res = run_bass_kernel_spmd(nc, in_maps, core_ids=list(range(N_CORES)))
        cur = np.stack([np.asarray(r["out"], dtype=np.float32) for r in res.results], axis=0)
    return cur
```
